# Optimizing a Trainium2 kernel written in Bass

```python
import math
import jax, jax.numpy as jnp
from jax import lax
import numpy as np

D_MODEL = 2048
BATCH = 4
SEQ = 4096
DEPTH = 4

N_REC = DEPTH // 2
N_ATTN = DEPTH - N_REC

LRU_WIDTH = D_MODEL
LRU_BLOCKS = 8
LRU_BLOCK_W = LRU_WIDTH // LRU_BLOCKS
CONV_WIDTH = 4
LRU_C = 8.0

HEAD_DIM = 64
N_HEADS = D_MODEL // HEAD_DIM
N_KV_HEADS = N_HEADS // 8
GROUP = N_HEADS // N_KV_HEADS
WINDOW = 128
BLOCK = 128

D_FF = ((8 * D_MODEL // 3 + 255) // 256) * 256

NORM_EPS = 1e-6
NEG_INF = -1e30

kernel_name = "yoco_rglru_swa_sink_hybrid"


def rms_norm(x, g):
    x32 = x.astype(jnp.float32)
    y = x32 * lax.rsqrt(jnp.mean(x32 * x32, axis=-1, keepdims=True) + NORM_EPS)
    return (y * g.astype(jnp.float32)).astype(x.dtype)


def swiglu_ffn(x, g, w_in, w_out):
    h = rms_norm(x, g)
    gate, up = jnp.split(h @ w_in, 2, axis=-1)
    return x + (jax.nn.silu(gate) * up) @ w_out


def causal_depthwise_conv(x, w, b):
    c = x.shape[-1]
    y = lax.conv_general_dilated(
        x, w[:, None, :], window_strides=(1,), padding=[(CONV_WIDTH - 1, 0)],
        dimension_numbers=("NWC", "WIO", "NWC"), feature_group_count=c)
    return y + b


def block_diag_linear(x, w, b):
    bsz, s, wd = x.shape
    xg = x.reshape(bsz, s, LRU_BLOCKS, LRU_BLOCK_W)
    y = jnp.einsum("bsnc,ncd->bsnd", xg, w).reshape(bsz, s, wd)
    return y + b


def rg_lru(xb, w_rg, b_rg, w_ig, b_ig, lam):
    r = jax.nn.sigmoid(block_diag_linear(xb, w_rg, b_rg).astype(jnp.float32))
    i = jax.nn.sigmoid(block_diag_linear(xb, w_ig, b_ig).astype(jnp.float32))
    log_a = -LRU_C * r * jax.nn.softplus(-lam.astype(jnp.float32))
    a = jnp.exp(log_a)
    u = jnp.sqrt(-jnp.expm1(2.0 * log_a)) * (i * xb.astype(jnp.float32))

    def combine(left, right):
        a_l, u_l = left
        a_r, u_r = right
        return a_l * a_r, a_r * u_l + u_r

    _, h = lax.associative_scan(combine, (a, u), axis=1)
    return h.astype(xb.dtype)


def recurrent_block(x, g, w_in, conv_w, conv_b, w_rg, b_rg, w_ig, b_ig, lam, w_out):
    h = rms_norm(x, g)
    y_branch, x_branch = jnp.split(h @ w_in, 2, axis=-1)
    xb = causal_depthwise_conv(x_branch, conv_w, conv_b)
    hs = rg_lru(xb, w_rg, b_rg, w_ig, b_ig, lam)
    return x + (hs * jax.nn.gelu(y_branch)) @ w_out


def shared_kv(x, g, w_kv, k_g):
    bsz, s, _ = x.shape
    nblk = s // BLOCK
    h = rms_norm(x, g)
    k, v = jnp.split(h @ w_kv, 2, axis=-1)
    k = rms_norm(k.reshape(bsz, s, N_KV_HEADS, HEAD_DIM), k_g)
    v = v.reshape(bsz, s, N_KV_HEADS, HEAD_DIM)

    def band(t):
        t = t.reshape(bsz, nblk, BLOCK, N_KV_HEADS, HEAD_DIM)
        prev = jnp.concatenate([jnp.zeros_like(t[:, :1]), t[:, :-1]], axis=1)
        return jnp.concatenate([prev, t], axis=2)

    return band(k), band(v)


def window_mask(nblk):
    q_pos = jnp.arange(BLOCK)[:, None] + BLOCK
    k_pos = jnp.arange(2 * BLOCK)[None, :]
    rel = q_pos - k_pos
    band_ok = (rel >= 0) & (rel < WINDOW)
    exists = (jnp.arange(nblk)[:, None, None] > 0) | (k_pos[None] >= BLOCK)
    return band_ok[None] & exists


def attn_block(x, g, w_q, q_g, sinks, w_o, k_band, v_band):
    bsz, s, _ = x.shape
    nblk = s // BLOCK
    h = rms_norm(x, g)
    q = rms_norm((h @ w_q).reshape(bsz, s, N_KV_HEADS, GROUP, HEAD_DIM), q_g)
    q = q.reshape(bsz, nblk, BLOCK, N_KV_HEADS, GROUP, HEAD_DIM)
    scores = jnp.einsum("bnqhgd,bnkhd->bnhgqk", q, k_band,
                        preferred_element_type=jnp.float32) * (1.0 / math.sqrt(HEAD_DIM))
    mask = window_mask(nblk)[None, :, None, None, :, :]
    scores = jnp.where(mask, scores, NEG_INF)
    sink = jnp.broadcast_to(
        sinks.astype(jnp.float32).reshape(N_KV_HEADS, GROUP)[None, None, :, :, None, None],
        scores.shape[:-1] + (1,))
    probs = jax.nn.softmax(jnp.concatenate([scores, sink], axis=-1), axis=-1)[..., :-1]
    out = jnp.einsum("bnhgqk,bnkhd->bnqhgd", probs.astype(v_band.dtype), v_band)
    return x + out.reshape(bsz, s, N_HEADS * HEAD_DIM) @ w_o


def setup_inputs(seed: int = 0) -> dict:
    key = jax.random.key(seed)
    ks = jax.random.split(key, 24)
    f32 = jnp.float32

    def nrm(k, shape, fan_in):
        return jax.random.normal(k, shape, f32) * (fan_in ** -0.5)

    def gain(k, shape):
        return 1.0 + 0.02 * jax.random.normal(k, shape, f32)

    def bias(k, shape):
        return 0.01 * jax.random.normal(k, shape, f32)

    a8 = jax.random.uniform(ks[13], (N_REC, LRU_WIDTH), f32, 0.9, 0.999)
    s_lam = a8 ** (1.0 / LRU_C)
    lru_lambda = jnp.log(s_lam) - jnp.log1p(-s_lam)

    return {
        "x": jax.random.normal(ks[0], (BATCH, SEQ, D_MODEL), f32),
        "norm1_g": gain(ks[1], (DEPTH, D_MODEL)),
        "norm2_g": gain(ks[2], (DEPTH, D_MODEL)),
        "ffn_w_in": nrm(ks[3], (DEPTH, D_MODEL, 2 * D_FF), D_MODEL),
        "ffn_w_out": nrm(ks[4], (DEPTH, D_FF, D_MODEL), D_FF),
        "lru_w_in": nrm(ks[5], (N_REC, D_MODEL, 2 * LRU_WIDTH), D_MODEL),
        "lru_conv_w": nrm(ks[6], (N_REC, CONV_WIDTH, LRU_WIDTH), CONV_WIDTH),
        "lru_conv_b": bias(ks[7], (N_REC, LRU_WIDTH)),
        "lru_w_rg": nrm(ks[8], (N_REC, LRU_BLOCKS, LRU_BLOCK_W, LRU_BLOCK_W), LRU_BLOCK_W),
        "lru_b_rg": bias(ks[9], (N_REC, LRU_WIDTH)),
        "lru_w_ig": nrm(ks[10], (N_REC, LRU_BLOCKS, LRU_BLOCK_W, LRU_BLOCK_W), LRU_BLOCK_W),
        "lru_b_ig": bias(ks[11], (N_REC, LRU_WIDTH)),
        "lru_lambda": lru_lambda,
        "lru_w_out": nrm(ks[12], (N_REC, LRU_WIDTH, D_MODEL), LRU_WIDTH),
        "kv_norm_g": gain(ks[14], (D_MODEL,)),
        "w_kv": nrm(ks[15], (D_MODEL, 2 * N_KV_HEADS * HEAD_DIM), D_MODEL),
        "k_norm_g": gain(ks[16], (HEAD_DIM,)),
        "w_q": nrm(ks[17], (N_ATTN, D_MODEL, N_HEADS * HEAD_DIM), D_MODEL),
        "q_norm_g": gain(ks[18], (N_ATTN, HEAD_DIM)),
        "sinks": 0.5 * jax.random.normal(ks[19], (N_ATTN, N_HEADS), f32),
        "w_o": nrm(ks[20], (N_ATTN, N_HEADS * HEAD_DIM, D_MODEL), N_HEADS * HEAD_DIM),
    }


def reference(x, norm1_g, norm2_g, ffn_w_in, ffn_w_out, lru_w_in, lru_conv_w, lru_conv_b,
              lru_w_rg, lru_b_rg, lru_w_ig, lru_b_ig, lru_lambda, lru_w_out,
              kv_norm_g, w_kv, k_norm_g, w_q, q_norm_g, sinks, w_o):
    k_band = None
    v_band = None
    for layer in range(DEPTH):
        if layer < N_REC:
            i = layer
            x = recurrent_block(x, norm1_g[layer], lru_w_in[i], lru_conv_w[i], lru_conv_b[i],
                                lru_w_rg[i], lru_b_rg[i], lru_w_ig[i], lru_b_ig[i],
                                lru_lambda[i], lru_w_out[i])
        else:
            if layer == N_REC:
                k_band, v_band = shared_kv(x, kv_norm_g, w_kv, k_norm_g)
            j = layer - N_REC
            x = attn_block(x, norm1_g[layer], w_q[j], q_norm_g[j], sinks[j], w_o[j],
                           k_band, v_band)
        x = swiglu_ffn(x, norm2_g[layer], ffn_w_in[layer], ffn_w_out[layer])
    return x
```

```python
import numpy as np
import concourse.bass as bass
import concourse.mybir as mybir
from concourse.bass_utils import run_bass_kernel_spmd

F32 = mybir.dt.float32
BF16 = mybir.dt.bfloat16
AF = mybir.ActivationFunctionType
ALU = mybir.AluOpType

D = 2048
NCH = 16
TT = 512
DFF = 5632
NFC = 44
GRP = 4
NGRP = NFC // GRP
EPS = 1e-6
SEQ = 4096
N_TILES = SEQ // TT
N_CORES = 4


class Sched:
    ENGS = ("pe", "act", "dve", "pool", "sp")

    def __init__(self, nc, same_engine_sync=("act", "dve", "pool")):
        self.nc = nc
        self.ops = {e: [] for e in self.ENGS}
        self.last_w = {}
        self.readers = {}
        self.dma_cnt = {}
        self.same = set(same_engine_sync)

    def _deps(self, reads, writes):
        deps = []
        for b in reads:
            t = self.last_w.get(b)
            if t is not None:
                deps.append(t)
        for b in writes:
            t = self.last_w.get(b)
            if t is not None:
                deps.append(t)
            deps.extend(self.readers.get(b, ()))
        return deps

    def _commit(self, tok, reads, writes):
        for b in writes:
            self.last_w[b] = tok
            self.readers[b] = []
        for b in reads:
            if b in writes:
                continue
            lst = self.readers.setdefault(b, [])
            if tok[0] == "eng":
                for i, t in enumerate(lst):
                    if t[0] == "eng" and t[1] == tok[1]:
                        lst[i] = tok
                        break
                else:
                    lst.append(tok)
            else:
                lst.append(tok)

    def op(self, e, fn, reads=(), writes=()):
        reads = tuple(reads); writes = tuple(writes)
        deps = self._deps(reads, writes)
        idx = len(self.ops[e])
        self.ops[e].append({"fn": fn, "deps": deps, "inc": False, "dma": None})
        tok = ("eng", e, idx)
        self._commit(tok, reads, writes)
        return tok

    def dma(self, q, fn, key, reads=(), writes=()):
        reads = tuple(reads); writes = tuple(writes)
        deps = self._deps(reads, writes)
        self.dma_cnt[key] = self.dma_cnt.get(key, 0) + 16
        tok = ("dma", key, self.dma_cnt[key])
        self.ops[q].append({"fn": fn, "deps": deps, "inc": False, "dma": tok})
        self._commit(tok, reads, writes)
        return tok

    def wait_all(self, e, toks):
        self.ops[e].append({"fn": None, "deps": list(toks), "inc": False, "dma": None})

    def emit(self):
        nc = self.nc
        for e in self.ENGS:
            for rec in self.ops[e]:
                nd = []
                for t in rec["deps"]:
                    if t[0] == "eng":
                        if t[1] == e and e not in self.same:
                            continue
                        self.ops[t[1]][t[2]]["inc"] = True
                    nd.append(t)
                rec["deps"] = nd
        val = {}
        for e in self.ENGS:
            c = 0
            for i, rec in enumerate(self.ops[e]):
                if rec["inc"] and rec["dma"] is None:
                    c += 1
                val[(e, i)] = c
        esem = {e: nc.alloc_semaphore(name=f"s_{e}") for e in self.ENGS}
        dsem = {k: nc.alloc_semaphore(name=f"d_{k}") for k in self.dma_cnt}
        self.n_instr = {e: 0 for e in self.ENGS}
        self.n_wait = {e: 0 for e in self.ENGS}

        def run_engine(e, engine):
            known = {}
            for i, rec in enumerate(self.ops[e]):
                need = {}
                for t in rec["deps"]:
                    if t[0] == "eng":
                        s = esem[t[1]]; v = val[(t[1], t[2])]; kk = ("e", t[1])
                    else:
                        s = dsem[t[1]]; v = t[2]; kk = ("d", t[1])
                    if known.get(kk, 0) >= v:
                        continue
                    if kk not in need or need[kk][1] < v:
                        need[kk] = (s, v)
                for kk, (s, v) in need.items():
                    engine.wait_ge(s, v)
                    known[kk] = v
                    self.n_wait[e] += 1
                if rec["fn"] is None:
                    continue
                ins = rec["fn"](engine)
                self.n_instr[e] += 1
                if rec["dma"] is not None:
                    ins.then_inc(dsem[rec["dma"][1]], 16)
                elif rec["inc"]:
                    ins.then_inc(esem[e], 1)

        with nc.Block() as block:
            @block.tensor
            def _(eng):
                run_engine("pe", eng)

            @block.scalar
            def _(eng):
                run_engine("act", eng)

            @block.vector
            def _(eng):
                run_engine("dve", eng)

            @block.gpsimd
            def _(eng):
                run_engine("pool", eng)

            @block.sync
            def _(eng):
                run_engine("sp", eng)


def _vec_layout():
    cols = {}
    off = 0

    def add(name, w):
        nonlocal off
        cols[name] = off
        off += w
    for l in range(4):
        add(f"n1g{l}", 16)
        add(f"n2g{l}", 16)
    add("kvg", 16)
    for l in range(2):
        for k in range(4):
            add(f"cw{l}_{k}", 16)
        add(f"cb{l}", 16)
        add(f"brg{l}", 16)
        add(f"big{l}", 16)
        add(f"lam{l}", 16)
    add("kg", 1)
    add("qg0", 1)
    add("qg1", 1)
    add("sk0", 32)
    add("sk1", 32)
    return cols, off


VCOL, NV = _vec_layout()


def build(NT, layers=4):
    nc = bass.Bass("TRN2", target_bir_lowering=False)
    S = Sched(nc)

    def dram(name, shape, kind="ExternalInput"):
        return nc.dram_tensor(name, list(shape), F32, kind=kind).ap()

    xT = dram("xT", [NT, 128, NCH * TT])
    oT = dram("oT", [NT, 128, NCH * TT], kind="ExternalOutput")
    vecs_d = dram("vecs", [128, NV])
    w_lin = dram("w_lin", [2, 32, 128, 2048])
    w_lout = dram("w_lout", [2, 16, 128, 2048])
    w_gate = dram("w_gate", [2, 8, 128, 1024])
    w_fin = dram("w_fin", [4, NFC, 128, 4096])
    w_fout = dram("w_fout", [4, NFC, 128, 2048])
    w_k = dram("w_k", [4, 128, 2048])
    w_v = dram("w_v", [128, 4096])
    w_q = dram("w_q", [2, 16, 128, 2048])
    w_o = dram("w_o", [2, 16, 128, 2048])

    sb = nc.alloc_sbuf_tensor
    resid = sb("resid", [128, NCH, TT], F32)
    hb = sb("hb", [128, NCH, TT], BF16)
    mt = sb("mt", [128, NCH, TT], BF16)
    actb = sb("actb", [128, 2, GRP, TT], BF16)
    NR4, NR8 = 6, 3
    ring4 = [sb(f"r4_{i}", [128, 2048], BF16) for i in range(NR4)]
    ring8 = [sb(f"r8_{i}", [128, 4096], BF16) for i in range(NR8)]
    vec = sb("vec", [128, NV], F32)
    ones_bf = sb("ones_bf", [128, 128], BF16)
    blk16 = sb("blk16", [128, 128], BF16)
    sqh = sb("sqh", [128, 2, TT], BF16)
    sql = sb("sql", [128, 2, TT], BF16)
    ones512 = sb("ones512", [128, 512], BF16)
    maskc = sb("maskc", [128, 512], BF16)
    maskp = sb("maskp", [128, 512], BF16)
    cneg = sb("cneg", [128, 2, 16], F32)
    cneg2 = sb("cneg2", [128, 2, 16], F32)
    esk = sb("esk", [128, 2, 32], F32)
    ctmp = sb("ctmp", [128, 4, 32], F32)
    rstd = sb("rstd", [128, TT], F32)
    lnt = rstd
    state = sb("state", [128, 2, 16], F32)
    halo = sb("halo", [128, 2, 16, 4], F32)
    scr = sb("scr", [128, 4 * (TT + 4) + 4 * TT], F32)
    xr = scr[:, 0:4 * (TT + 4)].rearrange("p (a b t) -> p a b t", a=2, b=2)
    xb = scr[:, 4 * (TT + 4):4 * (TT + 4) + 4 * TT].rearrange("p (a b t) -> p a b t", a=2, b=2)
    xbb = sb("xbb", [128, 2, 2, TT], BF16)
    t_ra = sb("t_ra", [128, 2, TT], F32)
    t_s = sb("t_s", [128, 2, TT], F32)
    t_iu = sb("t_iu", [128, 2, TT], F32)
    t_hs = sb("t_hs", [128, 2, TT], F32)
    t_gy = sb("t_gy", [128, 2, TT], F32)
    qT = scr.bitcast(BF16)[:, 0:NCH * TT].rearrange("p (c t) -> p c t", c=NCH)
    kT2 = sb("kT2", [128, 4, 128 + TT], BF16)
    vtok = sb("vtok", [128, 5, 256], BF16)
    pT = sb("pT", [128, 8, TT], BF16)
    sqt = sb("sqt", [128, 2, TT], F32)
    dent = sb("dent", [128, 2, TT], F32)
    ps = [nc.alloc_psum_tensor(f"ps{i}", [128, 512], F32) for i in range(8)]

    cnt = {"ps": 0, "r4": 0, "r8": 0}

    def nb():
        b = cnt["ps"] % 8
        cnt["ps"] += 1
        return b

    XK = [("x", c) for c in range(NCH)]
    HK = [("hb", c) for c in range(NCH)]

    def mkeys(c):
        return [("m", c, qb, par) for qb in range(4) for par in range(2)]
    MK = [k for c in range(NCH) for k in mkeys(c)]
    QK = [("q", c) for c in range(NCH)]

    def V(name, w=16):
        o = VCOL[name]
        return vec[:, o:o + w]

    def vcol(name, c):
        o = VCOL[name] + c
        return vec[:, o:o + 1]

    def load4(src, ncols=2048):
        s = cnt["r4"] % NR4
        cnt["r4"] += 1
        dst = ring4[s][:, 0:ncols]
        S.dma("pool", lambda e: e.dma_start(out=dst, in_=src), f"r4_{s}", writes=[("r4", s)])
        return s

    def load8(src):
        s = cnt["r8"] % NR8
        cnt["r8"] += 1
        S.dma("pool", lambda e: e.dma_start(out=ring8[s][:].rearrange("p (a b) -> p a b", b=2048),
                                            in_=src.rearrange("p (a b) -> p a b", b=2048)),
              f"r8_{s}", writes=[("r8", s)])
        return s

    S.dma("sp", lambda e: e.dma_start(out=vec[:], in_=vecs_d[:, :]), "vec", writes=["vec"])
    S.op("dve", lambda e: e.memset(ones_bf[:], 1.0), writes=["ones_bf"])
    S.op("dve", lambda e: e.memset(ones512[:], 1.0), writes=["ones512"])
    S.op("dve", lambda e: e.memset(blk16[:], 0.0), writes=["blk16"])
    S.op("dve", lambda e: e.memset(blk16[0:64, 0:64], 1.0), writes=["blk16"])
    S.op("dve", lambda e: e.memset(blk16[64:128, 64:128], 1.0), writes=["blk16"])
    S.op("dve", lambda e: e.memset(state[:], 0.0), writes=[("st", l, c) for l in range(2) for c in range(NCH)])
    S.op("dve", lambda e: e.memset(halo[:], 0.0), writes=[("halo", l, c) for l in range(2) for c in range(NCH)])
    S.op("dve", lambda e: e.memset(kT2[:], 0.0), writes=[("kT", h) for h in range(4)])
    S.op("dve", lambda e: e.memset(vtok[:], 0.0), writes=["vt"])
    S.op("pool", lambda e: e.affine_select(out=maskc[:], in_=ones512[:], pattern=[[0, 4], [1, 128]],
                                           compare_op=ALU.is_ge, fill=0.0, base=0, channel_multiplier=-1),
         reads=["ones512"], writes=["maskc"])
    S.op("pool", lambda e: e.affine_select(out=maskp[:], in_=ones512[:], pattern=[[0, 4], [-1, 128]],
                                           compare_op=ALU.is_ge, fill=0.0, base=-1, channel_multiplier=1),
         reads=["ones512"], writes=["maskp"])
    for l in range(2):
        lam = V(f"lam{l}")
        z = ctmp[:, 0, 0:16]; az = ctmp[:, 1, 0:16]; ee = ctmp[:, 2, 0:16]; rz = ctmp[:, 3, 0:16]
        S.op("dve", lambda e, z=z, lam=lam: e.tensor_scalar_mul(z, lam, -1.0), reads=["vec"], writes=["ctmp"])
        S.op("dve", lambda e, z=z, lam=lam, az=az: e.tensor_tensor(az, z, lam, ALU.max), reads=["vec", "ctmp"], writes=["ctmp"])
        S.op("act", lambda e, az=az, ee=ee: e.activation(out=ee, in_=az, func=AF.Exp, scale=-1.0), reads=["ctmp"], writes=["ctmp"])
        S.op("act", lambda e, ee=ee: e.activation(out=ee, in_=ee, func=AF.Ln, bias=1.0), reads=["ctmp"], writes=["ctmp"])
        S.op("dve", lambda e, z=z, rz=rz: e.tensor_scalar_max(rz, z, 0.0), reads=["ctmp"], writes=["ctmp"])
        S.op("dve", lambda e, rz=rz, ee=ee: e.tensor_tensor(rz, rz, ee, ALU.add), reads=["ctmp"], writes=["ctmp"])
        S.op("dve", lambda e, rz=rz, l=l: e.tensor_scalar_mul(cneg[:, l, :], rz, -8.0), reads=["ctmp"], writes=["cneg"])
        S.op("dve", lambda e, rz=rz, l=l: e.tensor_scalar_mul(cneg2[:, l, :], rz, -16.0), reads=["ctmp"], writes=["cneg"])
    for j in range(2):
        S.op("act", lambda e, j=j: e.activation(out=esk[:, j, :], in_=V(f"sk{j}", 32), func=AF.Exp), reads=["vec"], writes=["esk"])

    def rmsnorm(gname):
        S.op("act", lambda e: e.activation(out=hb[:].rearrange("p c t -> p (c t)"),
                                           in_=resid[:].rearrange("p c t -> p (c t)"), func=AF.Square),
             reads=XK, writes=HK)
        b = nb()
        for c in range(NCH):
            S.op("pe", lambda e, c=c, b=b: e.matmul(ps[b][:], lhsT=ones_bf[:], rhs=hb[:, c, :],
                                                   start=(c == 0), stop=(c == NCH - 1)),
                 reads=[("hb", c), "ones_bf"], writes=[("ps", b)])
        S.op("act", lambda e, b=b: e.activation(out=lnt[:], in_=ps[b][:], func=AF.Ln, scale=1.0 / D, bias=EPS),
             reads=[("ps", b)], writes=["lnt"])
        S.op("act", lambda e: e.activation(out=rstd[:], in_=lnt[:], func=AF.Exp, scale=-0.5),
             reads=["lnt"], writes=["rstd"])
        for c in range(NCH):
            S.op("dve", lambda e, c=c: e.scalar_tensor_tensor(out=hb[:, c, :], in0=resid[:, c, :],
                                                             scalar=vcol(gname, c), in1=rstd[:],
                                                             op0=ALU.mult, op1=ALU.mult),
                 reads=[("x", c), "rstd", "vec"], writes=[("hb", c)])

    def proj_chunk(wslot, src, srckeys, wring=None, woff=0):
        b = nb()
        wr = ring4 if wring is None else wring
        wkey = ("r4", wslot) if wring is None else ("r8", wslot)
        for k in range(NCH):
            S.op("pe", lambda e, k=k, b=b: e.matmul(ps[b][:], lhsT=wr[wslot][:, woff + k * 128: woff + (k + 1) * 128],
                                                   rhs=src[:, k, :], start=(k == 0), stop=(k == NCH - 1)),
                 reads=[wkey, srckeys[k]] if not isinstance(srckeys[k], list) else [wkey] + srckeys[k],
                 writes=[("ps", b)])
        return b

    def add_to_resid(b, oc):
        S.op("dve", lambda e, b=b, oc=oc: e.tensor_tensor(out=resid[:, oc, :], in0=resid[:, oc, :], in1=ps[b][:], op=ALU.add),
             reads=[("ps", b)], writes=[("x", oc)])

    def lru_layer(l):
        rmsnorm(f"n1g{l}")
        mkl = [mkeys(c) for c in range(NCH)]

        def xproj(j):
            par = j % 2
            banks = []
            for k2 in range(2):
                cc = 2 * j + k2
                s = load4(w_lin[l, 16 + cc])
                b = proj_chunk(s, hb, HK)
                banks.append(b)
                S.op("pool", lambda e, cc=cc, k2=k2, par=par: e.tensor_copy(out=xr[:, par, k2, 0:3], in_=halo[:, l, cc, 0:3]),
                     reads=[("halo", l, cc)], writes=[("xr", par, k2)])
                S.op("act", lambda e, b=b, k2=k2, par=par: e.activation(out=xr[:, par, k2, 3:3 + TT], in_=ps[b][:], func=AF.Copy),
                     reads=[("ps", b)], writes=[("xr", par, k2)])
            return banks

        def conv(j):
            par = j % 2
            for k2 in range(2):
                cc = 2 * j + k2
                o = xb[:, par, k2, :]
                S.op("dve", lambda e, o=o, cc=cc, k2=k2, par=par: e.tensor_scalar(
                    o, xr[:, par, k2, 3:3 + TT], vcol(f"cw{l}_3", cc), vcol(f"cb{l}", cc), ALU.mult, ALU.add),
                    reads=[("xr", par, k2), "vec"], writes=[("xb", par, k2)])
                for k in range(3):
                    S.op("dve", lambda e, o=o, cc=cc, k2=k2, par=par, k=k: e.scalar_tensor_tensor(
                        out=o, in0=xr[:, par, k2, k:k + TT], scalar=vcol(f"cw{l}_{k}", cc), in1=o,
                        op0=ALU.mult, op1=ALU.add),
                        reads=[("xr", par, k2), "vec"], writes=[("xb", par, k2)])
                S.op("act", lambda e, o=o, k2=k2, par=par: e.activation(out=xbb[:, par, k2, :], in_=o, func=AF.Copy),
                     reads=[("xb", par, k2)], writes=[("xbb", par, k2)])
                S.op("pool", lambda e, cc=cc, k2=k2, par=par: e.tensor_copy(out=halo[:, l, cc, 0:3], in_=xr[:, par, k2, TT:TT + 3]),
                     reads=[("xr", par, k2)], writes=[("halo", l, cc)])

        def gates(j):
            par = j % 2
            s = load4(w_gate[l, j], ncols=1024)
            res = {}
            for g in range(2):
                for oc in range(2):
                    b = nb()
                    for ic in range(2):
                        off = ic * 512 + g * 256 + oc * 128
                        S.op("pe", lambda e, b=b, ic=ic, off=off, s=s, par=par: e.matmul(
                            ps[b][:], lhsT=ring4[s][:, off:off + 128], rhs=xbb[:, par, ic, :],
                            start=(ic == 0), stop=(ic == 1)),
                            reads=[("r4", s), ("xbb", par, ic)], writes=[("ps", b)])
                    res[(g, oc)] = b
            return res

        def gate_math(j, gb):
            par = j % 2
            for oc in range(2):
                cc = 2 * j + oc
                q = cc % 2
                br, bi = gb[(0, oc)], gb[(1, oc)]
                ra = t_ra[:, q, :]; ss = t_s[:, q, :]; iu = t_iu[:, q, :]; hs = t_hs[:, q, :]
                S.op("act", lambda e, ra=ra, br=br, cc=cc: e.activation(out=ra, in_=ps[br][:], func=AF.Sigmoid, bias=vcol(f"brg{l}", cc)),
                     reads=[("ps", br), "vec"], writes=[("ra", q)])
                S.op("act", lambda e, iu=iu, bi=bi, cc=cc: e.activation(out=iu, in_=ps[bi][:], func=AF.Sigmoid, bias=vcol(f"big{l}", cc)),
                     reads=[("ps", bi), "vec"], writes=[("iu", q)])
                S.op("act", lambda e, ra=ra, ss=ss, cc=cc: e.activation(out=ss, in_=ra, func=AF.Exp, scale=cneg2[:, l, cc:cc + 1]),
                     reads=[("ra", q), "cneg"], writes=[("s", q)])
                S.op("act", lambda e, ra=ra, cc=cc: e.activation(out=ra, in_=ra, func=AF.Exp, scale=cneg[:, l, cc:cc + 1]),
                     reads=[("ra", q), "cneg"], writes=[("ra", q)])
                S.op("act", lambda e, ss=ss: e.activation(out=ss, in_=ss, func=AF.Ln, scale=-1.0, bias=1.0 + 1e-7),
                     reads=[("s", q)], writes=[("s", q)])
                S.op("act", lambda e, ss=ss: e.activation(out=ss, in_=ss, func=AF.Exp, scale=0.5),
                     reads=[("s", q)], writes=[("s", q)])
                S.op("dve", lambda e, iu=iu, oc=oc, par=par: e.tensor_tensor(out=iu, in0=iu, in1=xb[:, par, oc, :], op=ALU.mult),
                     reads=[("iu", q), ("xb", par, oc)], writes=[("iu", q)])
                S.op("dve", lambda e, iu=iu, ss=ss: e.tensor_tensor(out=iu, in0=iu, in1=ss, op=ALU.mult),
                     reads=[("iu", q), ("s", q)], writes=[("iu", q)])
                S.op("dve", lambda e, hs=hs, ra=ra, iu=iu, cc=cc: e.tensor_tensor_scan(
                    out=hs, data0=ra, data1=iu, initial=state[:, l, cc:cc + 1], op0=ALU.mult, op1=ALU.add),
                    reads=[("ra", q), ("iu", q), ("st", l, cc)], writes=[("hs", q)])
                S.op("dve", lambda e, hs=hs, cc=cc: e.tensor_copy(out=state[:, l, cc:cc + 1], in_=hs[:, TT - 1:TT]),
                     reads=[("hs", q)], writes=[("st", l, cc)])

        def yproj(j):
            for k2 in range(2):
                cc = 2 * j + k2
                q = cc % 2
                s = load4(w_lin[l, cc])
                b = proj_chunk(s, hb, HK)
                S.op("act", lambda e, b=b, q=q: e.activation(out=t_gy[:, q, :], in_=ps[b][:], func=AF.Gelu_apprx_tanh),
                     reads=[("ps", b)], writes=[("gy", q)])
                S.op("dve", lambda e, cc=cc, q=q: e.tensor_tensor(out=mt[:, cc, :], in0=t_gy[:, q, :], in1=t_hs[:, q, :], op=ALU.mult),
                     reads=[("gy", q), ("hs", q)], writes=mkl[cc])

        xproj(0)
        for j in range(8):
            conv(j)
            if j + 1 < 8:
                xproj(j + 1)
            gb = gates(j)
            gate_math(j, gb)
            yproj(j)
        for oc in range(NCH):
            s = load4(w_lout[l, oc])
            b = proj_chunk(s, mt, mkl)
            add_to_resid(b, oc)

    def ffn_layer(l):
        rmsnorm(f"n2g{l}")

        def phase1(g):
            slot = g % 2
            for f in range(GRP):
                fc = g * GRP + f
                s = load8(w_fin[l, fc])
                bg = proj_chunk(s, hb, HK, wring=ring8, woff=0)
                bu = proj_chunk(s, hb, HK, wring=ring8, woff=2048)
                q = fc % 2
                S.op("act", lambda e, bg=bg, q=q: e.activation(out=sqt[:, q, :], in_=ps[bg][:], func=AF.Silu),
                     reads=[("ps", bg)], writes=[("sq", q)])
                S.op("dve", lambda e, bu=bu, q=q, slot=slot, f=f: e.tensor_tensor(
                    out=actb[:, slot, f, :], in0=sqt[:, q, :], in1=ps[bu][:], op=ALU.mult),
                    reads=[("sq", q), ("ps", bu)], writes=[("actb", slot, f)])

        def phase2(g):
            slot = g % 2
            ws = [load4(w_fout[l, g * GRP + f]) for f in range(GRP)]
            for oc in range(NCH):
                b = nb()
                for f in range(GRP):
                    S.op("pe", lambda e, b=b, f=f, oc=oc, slot=slot, ws=ws: e.matmul(
                        ps[b][:], lhsT=ring4[ws[f]][:, oc * 128:(oc + 1) * 128], rhs=actb[:, slot, f, :],
                        start=(f == 0), stop=(f == GRP - 1)),
                        reads=[("r4", ws[f]), ("actb", slot, f)], writes=[("ps", b)])
                add_to_resid(b, oc)

        phase1(0)
        for g in range(NGRP):
            if g + 1 < NGRP:
                phase1(g + 1)
            phase2(g)

    def headnorm(b, gcol_ap, out_ap, out_keys, q):
        S.op("act", lambda e, b=b, q=q: e.activation(out=sqt[:, q, :], in_=ps[b][:], func=AF.Square),
             reads=[("ps", b)], writes=[("sq", q)])
        S.op("act", lambda e, q=q: e.activation(out=sqh[:, q, :], in_=sqt[:, q, :], func=AF.Copy),
             reads=[("sq", q)], writes=[("sqh", q)])
        S.op("dve", lambda e, q=q: e.tensor_tensor(out=sql[:, q, :], in0=sqt[:, q, :], in1=sqh[:, q, :], op=ALU.subtract),
             reads=[("sq", q), ("sqh", q)], writes=[("sql", q)])
        b2 = nb()
        S.op("pe", lambda e, b2=b2, q=q: e.matmul(ps[b2][:], lhsT=blk16[:], rhs=sqh[:, q, :], start=True, stop=False),
             reads=[("sqh", q), "blk16"], writes=[("ps", b2)])
        S.op("pe", lambda e, b2=b2, q=q: e.matmul(ps[b2][:], lhsT=blk16[:], rhs=sql[:, q, :], start=False, stop=True),
             reads=[("sql", q), "blk16"], writes=[("ps", b2)])
        S.op("act", lambda e, b2=b2, q=q: e.activation(out=dent[:, q, :], in_=ps[b2][:], func=AF.Ln, scale=1.0 / 64, bias=EPS),
             reads=[("ps", b2)], writes=[("dent", q)])
        S.op("act", lambda e, q=q: e.activation(out=dent[:, q, :], in_=dent[:, q, :], func=AF.Exp, scale=-0.5),
             reads=[("dent", q)], writes=[("dent", q)])
        S.op("dve", lambda e, b=b, q=q: e.scalar_tensor_tensor(out=out_ap, in0=ps[b][:], scalar=gcol_ap, in1=dent[:, q, :],
                                                              op0=ALU.mult, op1=ALU.mult),
             reads=[("ps", b), ("dent", q), "vec"], writes=out_keys)

    def kv_tile():
        rmsnorm("kvg")
        for h in range(4):
            s = load4(w_k[h])
            b = proj_chunk(s, hb, HK)
            headnorm(b, V("kg", 1), kT2[:, h, 128:128 + TT], [("kT", h)], h % 2)
        s = load8(w_v)
        for tb in range(4):
            b = nb()
            for k in range(NCH):
                S.op("pe", lambda e, b=b, k=k, tb=tb, s=s: e.matmul(
                    ps[b][:, 0:256], lhsT=hb[:, k, tb * 128:(tb + 1) * 128], rhs=ring8[s][:, k * 256:(k + 1) * 256],
                    start=(k == 0), stop=(k == NCH - 1)),
                    reads=[("r8", s), ("hb", k)], writes=[("ps", b)])
            S.op("act", lambda e, b=b, tb=tb: e.activation(out=vtok[:, 1 + tb, :], in_=ps[b][:, 0:256], func=AF.Copy),
                 reads=[("ps", b)], writes=["vt"])

    def kv_shift():
        for h in range(4):
            S.op("pool", lambda e, h=h: e.tensor_copy(out=kT2[:, h, 0:128], in_=kT2[:, h, TT:TT + 128]),
                 reads=[("kT", h)], writes=[("kT", h)])
        S.op("pool", lambda e: e.tensor_copy(out=vtok[:, 0, :], in_=vtok[:, 4, :]), reads=["vt"], writes=["vt"])

    def attn_layer(l, first_tile):
        j = l - 2
        rmsnorm(f"n1g{l}")
        for oc in range(NCH):
            s = load4(w_q[j, oc])
            b = proj_chunk(s, hb, HK)
            headnorm(b, V(f"qg{j}", 1), qT[:, oc, :], [("q", oc)], oc % 2)

        def scores(qb, h, up):
            out = []
            for par in range(2):
                pr = slice(par * 64, par * 64 + 64)
                rhs = qT[pr, h * 4:h * 4 + 4, qb * 128:(qb + 1) * 128]
                lst = []
                for kb in range(2):
                    if kb == 0 and first_tile and qb == 0:
                        continue
                    blk = qb + kb
                    b = nb()
                    S.op("pe", lambda e, b=b, pr=pr, rhs=rhs, blk=blk, h=h: e.matmul(
                        ps[b][:].rearrange("p (a q) -> p a q", a=4), lhsT=kT2[pr, h, blk * 128:(blk + 1) * 128], rhs=rhs,
                        start=True, stop=True),
                        reads=[("kT", h)] + [("q", h * 4 + i) for i in range(4)], writes=[("ps", b)])
                    sl = up * 4 + par * 2 + kb
                    S.op("act", lambda e, b=b, sl=sl: e.activation(out=pT[:, sl, :], in_=ps[b][:], func=AF.Exp, scale=0.125),
                         reads=[("ps", b)], writes=[("pT", sl)])
                    mk = maskp if kb == 0 else maskc
                    S.op("dve", lambda e, sl=sl, mk=mk: e.tensor_tensor(out=pT[:, sl, :], in0=pT[:, sl, :], in1=mk[:], op=ALU.mult),
                         reads=[("pT", sl), "maskc", "maskp"], writes=[("pT", sl)])
                    lst.append((blk, sl))
                out.append((par, lst))
            return out

        def pv(qb, h, sc):
            bn = nb()
            bd = nb()
            for par, lst in sc:
                pr = slice(par * 64, par * 64 + 64)
                for i, (blk, sl) in enumerate(lst):
                    S.op("pe", lambda e, bn=bn, pr=pr, blk=blk, sl=sl, i=i, n=len(lst), h=h: e.matmul(
                        ps[bn][pr, :], lhsT=vtok[:, blk, h * 64:(h + 1) * 64], rhs=pT[:, sl, :],
                        start=(i == 0), stop=(i == n - 1)),
                        reads=["vt", ("pT", sl)], writes=[("ps", bn)])
                for i, (blk, sl) in enumerate(lst):
                    S.op("pe", lambda e, bd=bd, pr=pr, sl=sl, i=i, n=len(lst): e.matmul(
                        ps[bd][pr, :], lhsT=ones_bf[:, 0:64], rhs=pT[:, sl, :],
                        start=(i == 0), stop=(i == n - 1)),
                        reads=["ones_bf", ("pT", sl)], writes=[("ps", bd)])
            for par, lst in sc:
                pr = slice(par * 64, par * 64 + 64)
                q = par
                es = esk[pr, j, h * 8 + par:h * 8 + 8:2].unsqueeze(2).broadcast_to([64, 4, 128])
                S.op("dve", lambda e, bd=bd, pr=pr, q=q, es=es: e.tensor_tensor(
                    out=dent[pr, q, :].rearrange("p (a q) -> p a q", a=4),
                    in0=ps[bd][pr, :].rearrange("p (a q) -> p a q", a=4), in1=es, op=ALU.add),
                    reads=[("ps", bd), "esk"], writes=[("dent", q)])
                S.op("dve", lambda e, pr=pr, q=q: e.reciprocal(out=dent[pr, q, :], in_=dent[pr, q, :]),
                     reads=[("dent", q)], writes=[("dent", q)])
                S.op("dve", lambda e, bn=bn, pr=pr, q=q, qb=qb, h=h: e.tensor_tensor(
                    out=mt[pr, h * 4:h * 4 + 4, qb * 128:(qb + 1) * 128],
                    in0=ps[bn][pr, :].rearrange("p (a q) -> p a q", a=4),
                    in1=dent[pr, q, :].rearrange("p (a q) -> p a q", a=4), op=ALU.mult),
                    reads=[("ps", bn), ("dent", q)], writes=[("m", h * 4 + i, qb, par) for i in range(4)])

        units = [(qb, h) for qb in range(4) for h in range(4)]
        sc_cur = scores(*units[0], 0)
        for i, (qb, h) in enumerate(units):
            sc_next = None
            if i + 1 < len(units):
                sc_next = scores(*units[i + 1], (i + 1) % 2)
            pv(qb, h, sc_cur)
            sc_cur = sc_next
        mkl = [mkeys(c) for c in range(NCH)]
        for oc in range(NCH):
            s = load4(w_o[j, oc])
            b = proj_chunk(s, mt, mkl)
            add_to_resid(b, oc)

    out_toks = []
    for ti in range(NT):
        S.dma("sp", lambda e, ti=ti: e.dma_start(out=resid[:].rearrange("p c t -> p (c t)"), in_=xT[ti]),
              "xin", writes=XK)
        for l in range(layers):
            if l < 2:
                lru_layer(l)
            else:
                if l == 2:
                    kv_tile()
                attn_layer(l, first_tile=(ti == 0))
            ffn_layer(l)
        if layers > 2:
            kv_shift()
        t = S.dma("sp", lambda e, ti=ti: e.dma_start(out=oT[ti], in_=resid[:].rearrange("p c t -> p (c t)")),
                  "xout", reads=XK)
        out_toks.append(t)
    S.wait_all("sp", out_toks)
    S.emit()
    return nc, S


def _chunked(W):
    K, N = W.shape
    a = W.reshape(K // 128, 128, N // 128, 128)
    return np.ascontiguousarray(a.transpose(2, 1, 0, 3)).reshape(N // 128, 128, (K // 128) * 128)


def _v16(v):
    return np.ascontiguousarray(np.asarray(v, np.float32).reshape(16, 128).T)


def prep_weights(inp):
    f = lambda a: np.asarray(a, np.float32)
    vec = np.zeros((128, NV), np.float32)

    def put(name, arr):
        o = VCOL[name]
        vec[:, o:o + arr.shape[1]] = arr
    for l in range(4):
        put(f"n1g{l}", _v16(inp["norm1_g"][l]))
        put(f"n2g{l}", _v16(inp["norm2_g"][l]))
    put("kvg", _v16(inp["kv_norm_g"]))
    for l in range(2):
        for k in range(4):
            put(f"cw{l}_{k}", _v16(inp["lru_conv_w"][l][k]))
        put(f"cb{l}", _v16(inp["lru_conv_b"][l]))
        put(f"brg{l}", _v16(inp["lru_b_rg"][l]))
        put(f"big{l}", _v16(inp["lru_b_ig"][l]))
        put(f"lam{l}", _v16(inp["lru_lambda"][l]))
    put("kg", np.tile(f(inp["k_norm_g"]), 2)[:, None])
    for j in range(2):
        put(f"qg{j}", np.tile(f(inp["q_norm_g"][j]), 2)[:, None])
        put(f"sk{j}", np.tile(f(inp["sinks"][j])[None, :], (128, 1)))
    w = {"vecs": vec}
    w["w_lin"] = np.stack([_chunked(f(inp["lru_w_in"][l])) for l in range(2)])
    w["w_lout"] = np.stack([_chunked(f(inp["lru_w_out"][l])) for l in range(2)])
    rg = f(inp["lru_w_rg"]).reshape(2, 8, 2, 128, 256)
    ig = f(inp["lru_w_ig"]).reshape(2, 8, 2, 128, 256)
    gt = np.stack([rg, ig], axis=3)
    w["w_gate"] = np.ascontiguousarray(gt.transpose(0, 1, 4, 2, 3, 5)).reshape(2, 8, 128, 1024)
    fin = []
    for l in range(4):
        W = f(inp["ffn_w_in"][l])
        g = _chunked(W[:, :DFF])
        u = _chunked(W[:, DFF:])
        fin.append(np.concatenate([g, u], axis=2))
    w["w_fin"] = np.stack(fin)
    w["w_fout"] = np.ascontiguousarray(f(inp["ffn_w_out"]).reshape(4, NFC, 128, 2048))
    wkv = f(inp["w_kv"])
    wk = wkv[:, :256].reshape(2048, 4, 64)
    wk2 = np.concatenate([wk, wk], axis=2)
    w["w_k"] = np.stack([_chunked(np.ascontiguousarray(wk2[:, h, :]))[0] for h in range(4)])
    wv = wkv[:, 256:]
    w["w_v"] = np.ascontiguousarray(wv.reshape(16, 128, 256).transpose(1, 0, 2)).reshape(128, 4096)
    w["w_q"] = np.stack([_chunked(f(inp["w_q"][j])) for j in range(2)])
    w["w_o"] = np.stack([_chunked(f(inp["w_o"][j])) for j in range(2)])
    return w


def to_tiles(xseq, NT):
    a = xseq.reshape(NT, TT, NCH, 128)
    return np.ascontiguousarray(a.transpose(0, 3, 2, 1)).reshape(NT, 128, NCH * TT)


def from_tiles(o, NT):
    a = o.reshape(NT, 128, NCH, TT)
    return np.ascontiguousarray(a.transpose(0, 3, 2, 1)).reshape(NT * TT, D)


_CACHE = {}


def kernel(**inputs):
    x = np.asarray(inputs["x"], np.float32)
    B = x.shape[0]
    w = prep_weights(inputs)
    if "nc" not in _CACHE:
        _CACHE["nc"] = build(N_TILES)[0]
    nc = _CACHE["nc"]
    in_maps = []
    for c in range(B):
        m = dict(w)
        m["xT"] = to_tiles(x[c], N_TILES)
        in_maps.append(m)
    res = run_bass_kernel_spmd(nc, in_maps, core_ids=list(range(B)))
    out = np.stack([from_tiles(res.results[c]["oT"], N_TILES) for c in range(B)])
    return out.astype(np.float32)
```

```python
import numpy as np
import concourse.bass as bass
import concourse.mybir as mybir
from concourse.bass_utils import run_bass_kernel_spmd

F32 = mybir.dt.float32
BF16 = mybir.dt.bfloat16
AF = mybir.ActivationFunctionType
ALU = mybir.AluOpType

D = 2048
NCH = 16
TT = 512
DFF = 5632
NFC = 44
GRP = 4
NGRP = NFC // GRP
EPS = 1e-6
SEQ = 4096
N_TILES = SEQ // TT
N_CORES = 8


class Sched:
    ENGS = ("pe", "act", "dve", "pool", "sp")

    def __init__(self, nc, same_engine_sync=("act", "dve", "pool")):
        self.nc = nc
        self.ops = {e: [] for e in self.ENGS}
        self.last_w = {}
        self.readers = {}
        self.dma_cnt = {}
        self.same = set(same_engine_sync)

    def _deps(self, reads, writes):
        deps = []
        for b in reads:
            t = self.last_w.get(b)
            if t is not None:
                deps.append(t)
        for b in writes:
            t = self.last_w.get(b)
            if t is not None:
                deps.append(t)
            deps.extend(self.readers.get(b, ()))
        return deps

    def _commit(self, tok, reads, writes):
        for b in writes:
            self.last_w[b] = tok
            self.readers[b] = []
        for b in reads:
            if b in writes:
                continue
            lst = self.readers.setdefault(b, [])
            if tok[0] == "eng":
                for i, t in enumerate(lst):
                    if t[0] == "eng" and t[1] == tok[1]:
                        lst[i] = tok
                        break
                else:
                    lst.append(tok)
            else:
                lst.append(tok)

    def op(self, e, fn, reads=(), writes=()):
        reads = tuple(reads); writes = tuple(writes)
        deps = self._deps(reads, writes)
        idx = len(self.ops[e])
        self.ops[e].append({"fn": fn, "deps": deps, "inc": False, "dma": None})
        tok = ("eng", e, idx)
        self._commit(tok, reads, writes)
        return tok

    def dma(self, q, fn, key, reads=(), writes=()):
        reads = tuple(reads); writes = tuple(writes)
        deps = self._deps(reads, writes)
        self.dma_cnt[key] = self.dma_cnt.get(key, 0) + 16
        tok = ("dma", key, self.dma_cnt[key])
        self.ops[q].append({"fn": fn, "deps": deps, "inc": False, "dma": tok})
        self._commit(tok, reads, writes)
        return tok

    def wait_all(self, e, toks):
        self.ops[e].append({"fn": None, "deps": list(toks), "inc": False, "dma": None})

    def emit(self):
        nc = self.nc
        for e in self.ENGS:
            for rec in self.ops[e]:
                nd = []
                for t in rec["deps"]:
                    if t[0] == "eng":
                        if t[1] == e and e not in self.same:
                            continue
                        self.ops[t[1]][t[2]]["inc"] = True
                    nd.append(t)
                rec["deps"] = nd
        val = {}
        for e in self.ENGS:
            c = 0
            for i, rec in enumerate(self.ops[e]):
                if rec["inc"] and rec["dma"] is None:
                    c += 1
                val[(e, i)] = c
        esem = {e: nc.alloc_semaphore(name=f"s_{e}") for e in self.ENGS}
        dsem = {k: nc.alloc_semaphore(name=f"d_{k}") for k in self.dma_cnt}
        self.n_instr = {e: 0 for e in self.ENGS}
        self.n_wait = {e: 0 for e in self.ENGS}

        def run_engine(e, engine):
            known = {}
            for i, rec in enumerate(self.ops[e]):
                need = {}
                for t in rec["deps"]:
                    if t[0] == "eng":
                        s = esem[t[1]]; v = val[(t[1], t[2])]; kk = ("e", t[1])
                    else:
                        s = dsem[t[1]]; v = t[2]; kk = ("d", t[1])
                    if known.get(kk, 0) >= v:
                        continue
                    if kk not in need or need[kk][1] < v:
                        need[kk] = (s, v)
                for kk, (s, v) in need.items():
                    engine.wait_ge(s, v)
                    known[kk] = v
                    self.n_wait[e] += 1
                if rec["fn"] is None:
                    continue
                ins = rec["fn"](engine)
                self.n_instr[e] += 1
                if rec["dma"] is not None:
                    ins.then_inc(dsem[rec["dma"][1]], 16)
                elif rec["inc"]:
                    ins.then_inc(esem[e], 1)

        with nc.Block() as block:
            @block.tensor
            def _(eng):
                run_engine("pe", eng)

            @block.scalar
            def _(eng):
                run_engine("act", eng)

            @block.vector
            def _(eng):
                run_engine("dve", eng)

            @block.gpsimd
            def _(eng):
                run_engine("pool", eng)

            @block.sync
            def _(eng):
                run_engine("sp", eng)


def _vec_layout():
    cols = {}
    off = 0

    def add(name, w):
        nonlocal off
        cols[name] = off
        off += w
    for l in range(4):
        add(f"n1g{l}", 16)
        add(f"n2g{l}", 16)
    add("kvg", 16)
    for l in range(2):
        for k in range(4):
            add(f"cw{l}_{k}", 16)
        add(f"cb{l}", 16)
        add(f"brg{l}", 16)
        add(f"big{l}", 16)
        add(f"lam{l}", 16)
    add("kg", 1)
    add("qg0", 1)
    add("qg1", 1)
    add("sk0", 32)
    add("sk1", 32)
    add("flag", 1)
    return cols, off


VCOL, NV = _vec_layout()


def build(NTP, NTM, layers=4):
    nc = bass.Bass("TRN2", target_bir_lowering=False)
    S = Sched(nc)

    def dram(name, shape, kind="ExternalInput"):
        return nc.dram_tensor(name, list(shape), F32, kind=kind).ap()

    xT = dram("xT", [NTP + NTM, 128, NCH * TT])
    oT = dram("oT", [NTM, 128, NCH * TT], kind="ExternalOutput")
    vecs_d = dram("vecs", [128, NV])
    w_lin = dram("w_lin", [2, 32, 128, 2048])
    w_lout = dram("w_lout", [2, 16, 128, 2048])
    w_gate = dram("w_gate", [2, 8, 128, 1024])
    w_fin = dram("w_fin", [4, NFC, 128, 4096])
    w_fout = dram("w_fout", [4, NFC, 128, 2048])
    w_k = dram("w_k", [4, 128, 2048])
    w_v = dram("w_v", [128, 4096])
    w_q = dram("w_q", [2, 16, 128, 2048])
    w_o = dram("w_o", [2, 16, 128, 2048])

    sb = nc.alloc_sbuf_tensor
    resid = sb("resid", [128, NCH, TT], F32)
    hb = sb("hb", [128, NCH, TT], BF16)
    mt = sb("mt", [128, NCH, TT], BF16)
    actb = sb("actb", [128, 2, GRP, TT], BF16)
    NR4, NR8 = 6, 3
    ring4 = [sb(f"r4_{i}", [128, 2048], BF16) for i in range(NR4)]
    ring8 = [sb(f"r8_{i}", [128, 4096], BF16) for i in range(NR8)]
    vec = sb("vec", [128, NV], F32)
    ones_bf = sb("ones_bf", [128, 128], BF16)
    blk16 = sb("blk16", [128, 128], BF16)
    sqh = sb("sqh", [128, 2, TT], BF16)
    sql = sb("sql", [128, 2, TT], BF16)
    ones512 = sb("ones512", [128, 512], BF16)
    maskc = sb("maskc", [128, 512], BF16)
    maskp = sb("maskp", [128, 512], BF16)
    maskpf = sb("maskpf", [128, 512], BF16)
    cneg = sb("cneg", [128, 2, 16], F32)
    cneg2 = sb("cneg2", [128, 2, 16], F32)
    esk = sb("esk", [128, 2, 32], F32)
    ctmp = sb("ctmp", [128, 4, 32], F32)
    rstd = sb("rstd", [128, TT], F32)
    lnt = rstd
    state = sb("state", [128, 2, 16], F32)
    halo = sb("halo", [128, 2, 16, 4], F32)
    scr = sb("scr", [128, 4 * (TT + 4) + 4 * TT], F32)
    xr = scr[:, 0:4 * (TT + 4)].rearrange("p (a b t) -> p a b t", a=2, b=2)
    xb = scr[:, 4 * (TT + 4):4 * (TT + 4) + 4 * TT].rearrange("p (a b t) -> p a b t", a=2, b=2)
    xbb = sb("xbb", [128, 2, 2, TT], BF16)
    t_ra = sb("t_ra", [128, 2, TT], F32)
    t_s = sb("t_s", [128, 2, TT], F32)
    t_iu = sb("t_iu", [128, 2, TT], F32)
    t_hs = sb("t_hs", [128, 2, TT], F32)
    t_gy = sb("t_gy", [128, 2, TT], F32)
    qT = scr.bitcast(BF16)[:, 0:NCH * TT].rearrange("p (c t) -> p c t", c=NCH)
    kT2 = sb("kT2", [128, 4, 128 + TT], BF16)
    vtok = sb("vtok", [128, 5, 256], BF16)
    pT = sb("pT", [128, 8, TT], BF16)
    sqt = sb("sqt", [128, 2, TT], F32)
    dent = sb("dent", [128, 2, TT], F32)
    ps = [nc.alloc_psum_tensor(f"ps{i}", [128, 512], F32) for i in range(8)]

    cnt = {"ps": 0, "r4": 0, "r8": 0}

    def nb():
        b = cnt["ps"] % 8
        cnt["ps"] += 1
        return b

    XK = [("x", c) for c in range(NCH)]
    HK = [("hb", c) for c in range(NCH)]

    def mkeys(c):
        return [("m", c, qb, par) for qb in range(4) for par in range(2)]
    MK = [k for c in range(NCH) for k in mkeys(c)]
    QK = [("q", c) for c in range(NCH)]

    def V(name, w=16):
        o = VCOL[name]
        return vec[:, o:o + w]

    def vcol(name, c):
        o = VCOL[name] + c
        return vec[:, o:o + 1]

    def load4(src, ncols=2048):
        s = cnt["r4"] % NR4
        cnt["r4"] += 1
        dst = ring4[s][:, 0:ncols]
        S.dma("pool", lambda e: e.dma_start(out=dst, in_=src), f"r4_{s}", writes=[("r4", s)])
        return s

    def load8(src):
        s = cnt["r8"] % NR8
        cnt["r8"] += 1
        S.dma("pool", lambda e: e.dma_start(out=ring8[s][:].rearrange("p (a b) -> p a b", b=2048),
                                            in_=src.rearrange("p (a b) -> p a b", b=2048)),
              f"r8_{s}", writes=[("r8", s)])
        return s

    S.dma("sp", lambda e: e.dma_start(out=vec[:], in_=vecs_d[:, :]), "vec", writes=["vec"])
    S.op("dve", lambda e: e.memset(ones_bf[:], 1.0), writes=["ones_bf"])
    S.op("dve", lambda e: e.memset(ones512[:], 1.0), writes=["ones512"])
    S.op("dve", lambda e: e.memset(blk16[:], 0.0), writes=["blk16"])
    S.op("dve", lambda e: e.memset(blk16[0:64, 0:64], 1.0), writes=["blk16"])
    S.op("dve", lambda e: e.memset(blk16[64:128, 64:128], 1.0), writes=["blk16"])
    S.op("dve", lambda e: e.memset(state[:], 0.0), writes=[("st", l, c) for l in range(2) for c in range(NCH)])
    S.op("dve", lambda e: e.memset(halo[:], 0.0), writes=[("halo", l, c) for l in range(2) for c in range(NCH)])
    S.op("dve", lambda e: e.memset(kT2[:], 0.0), writes=[("kT", h) for h in range(4)])
    S.op("dve", lambda e: e.memset(vtok[:], 0.0), writes=["vt"])
    S.op("pool", lambda e: e.affine_select(out=maskc[:], in_=ones512[:], pattern=[[0, 4], [1, 128]],
                                           compare_op=ALU.is_ge, fill=0.0, base=0, channel_multiplier=-1),
         reads=["ones512"], writes=["maskc"])
    S.op("pool", lambda e: e.affine_select(out=maskp[:], in_=ones512[:], pattern=[[0, 4], [-1, 128]],
                                           compare_op=ALU.is_ge, fill=0.0, base=-1, channel_multiplier=1),
         reads=["ones512"], writes=["maskp"])
    S.op("dve", lambda e: e.tensor_scalar_mul(maskpf[:], maskp[:], V("flag", 1)), reads=["maskp", "vec"], writes=["maskpf"])
    for l in range(2):
        lam = V(f"lam{l}")
        z = ctmp[:, 0, 0:16]; az = ctmp[:, 1, 0:16]; ee = ctmp[:, 2, 0:16]; rz = ctmp[:, 3, 0:16]
        S.op("dve", lambda e, z=z, lam=lam: e.tensor_scalar_mul(z, lam, -1.0), reads=["vec"], writes=["ctmp"])
        S.op("dve", lambda e, z=z, lam=lam, az=az: e.tensor_tensor(az, z, lam, ALU.max), reads=["vec", "ctmp"], writes=["ctmp"])
        S.op("act", lambda e, az=az, ee=ee: e.activation(out=ee, in_=az, func=AF.Exp, scale=-1.0), reads=["ctmp"], writes=["ctmp"])
        S.op("act", lambda e, ee=ee: e.activation(out=ee, in_=ee, func=AF.Ln, bias=1.0), reads=["ctmp"], writes=["ctmp"])
        S.op("dve", lambda e, z=z, rz=rz: e.tensor_scalar_max(rz, z, 0.0), reads=["ctmp"], writes=["ctmp"])
        S.op("dve", lambda e, rz=rz, ee=ee: e.tensor_tensor(rz, rz, ee, ALU.add), reads=["ctmp"], writes=["ctmp"])
        S.op("dve", lambda e, rz=rz, l=l: e.tensor_scalar_mul(cneg[:, l, :], rz, -8.0), reads=["ctmp"], writes=["cneg"])
        S.op("dve", lambda e, rz=rz, l=l: e.tensor_scalar_mul(cneg2[:, l, :], rz, -16.0), reads=["ctmp"], writes=["cneg"])
    for j in range(2):
        S.op("act", lambda e, j=j: e.activation(out=esk[:, j, :], in_=V(f"sk{j}", 32), func=AF.Exp), reads=["vec"], writes=["esk"])

    def rmsnorm(gname):
        S.op("act", lambda e: e.activation(out=hb[:].rearrange("p c t -> p (c t)"),
                                           in_=resid[:].rearrange("p c t -> p (c t)"), func=AF.Square),
             reads=XK, writes=HK)
        b = nb()
        for c in range(NCH):
            S.op("pe", lambda e, c=c, b=b: e.matmul(ps[b][:], lhsT=ones_bf[:], rhs=hb[:, c, :],
                                                   start=(c == 0), stop=(c == NCH - 1)),
                 reads=[("hb", c), "ones_bf"], writes=[("ps", b)])
        S.op("act", lambda e, b=b: e.activation(out=lnt[:], in_=ps[b][:], func=AF.Ln, scale=1.0 / D, bias=EPS),
             reads=[("ps", b)], writes=["lnt"])
        S.op("act", lambda e: e.activation(out=rstd[:], in_=lnt[:], func=AF.Exp, scale=-0.5),
             reads=["lnt"], writes=["rstd"])
        for c in range(NCH):
            S.op("dve", lambda e, c=c: e.scalar_tensor_tensor(out=hb[:, c, :], in0=resid[:, c, :],
                                                             scalar=vcol(gname, c), in1=rstd[:],
                                                             op0=ALU.mult, op1=ALU.mult),
                 reads=[("x", c), "rstd", "vec"], writes=[("hb", c)])

    def proj_chunk(wslot, src, srckeys, wring=None, woff=0):
        b = nb()
        wr = ring4 if wring is None else wring
        wkey = ("r4", wslot) if wring is None else ("r8", wslot)
        for k in range(NCH):
            S.op("pe", lambda e, k=k, b=b: e.matmul(ps[b][:], lhsT=wr[wslot][:, woff + k * 128: woff + (k + 1) * 128],
                                                   rhs=src[:, k, :], start=(k == 0), stop=(k == NCH - 1)),
                 reads=[wkey, srckeys[k]] if not isinstance(srckeys[k], list) else [wkey] + srckeys[k],
                 writes=[("ps", b)])
        return b

    def add_to_resid(b, oc):
        S.op("dve", lambda e, b=b, oc=oc: e.tensor_tensor(out=resid[:, oc, :], in0=resid[:, oc, :], in1=ps[b][:], op=ALU.add),
             reads=[("ps", b)], writes=[("x", oc)])

    def lru_layer(l, scan_only=False):
        rmsnorm(f"n1g{l}")
        mkl = [mkeys(c) for c in range(NCH)]

        def xproj(j):
            par = j % 2
            banks = []
            for k2 in range(2):
                cc = 2 * j + k2
                s = load4(w_lin[l, 16 + cc])
                b = proj_chunk(s, hb, HK)
                banks.append(b)
                S.op("pool", lambda e, cc=cc, k2=k2, par=par: e.tensor_copy(out=xr[:, par, k2, 0:3], in_=halo[:, l, cc, 0:3]),
                     reads=[("halo", l, cc)], writes=[("xr", par, k2)])
                S.op("act", lambda e, b=b, k2=k2, par=par: e.activation(out=xr[:, par, k2, 3:3 + TT], in_=ps[b][:], func=AF.Copy),
                     reads=[("ps", b)], writes=[("xr", par, k2)])
            return banks

        def conv(j):
            par = j % 2
            for k2 in range(2):
                cc = 2 * j + k2
                o = xb[:, par, k2, :]
                S.op("dve", lambda e, o=o, cc=cc, k2=k2, par=par: e.tensor_scalar(
                    o, xr[:, par, k2, 3:3 + TT], vcol(f"cw{l}_3", cc), vcol(f"cb{l}", cc), ALU.mult, ALU.add),
                    reads=[("xr", par, k2), "vec"], writes=[("xb", par, k2)])
                for k in range(3):
                    S.op("dve", lambda e, o=o, cc=cc, k2=k2, par=par, k=k: e.scalar_tensor_tensor(
                        out=o, in0=xr[:, par, k2, k:k + TT], scalar=vcol(f"cw{l}_{k}", cc), in1=o,
                        op0=ALU.mult, op1=ALU.add),
                        reads=[("xr", par, k2), "vec"], writes=[("xb", par, k2)])
                S.op("act", lambda e, o=o, k2=k2, par=par: e.activation(out=xbb[:, par, k2, :], in_=o, func=AF.Copy),
                     reads=[("xb", par, k2)], writes=[("xbb", par, k2)])
                S.op("pool", lambda e, cc=cc, k2=k2, par=par: e.tensor_copy(out=halo[:, l, cc, 0:3], in_=xr[:, par, k2, TT:TT + 3]),
                     reads=[("xr", par, k2)], writes=[("halo", l, cc)])

        def gates(j):
            par = j % 2
            s = load4(w_gate[l, j], ncols=1024)
            res = {}
            for g in range(2):
                for oc in range(2):
                    b = nb()
                    for ic in range(2):
                        off = ic * 512 + g * 256 + oc * 128
                        S.op("pe", lambda e, b=b, ic=ic, off=off, s=s, par=par: e.matmul(
                            ps[b][:], lhsT=ring4[s][:, off:off + 128], rhs=xbb[:, par, ic, :],
                            start=(ic == 0), stop=(ic == 1)),
                            reads=[("r4", s), ("xbb", par, ic)], writes=[("ps", b)])
                    res[(g, oc)] = b
            return res

        def gate_math(j, gb):
            par = j % 2
            for oc in range(2):
                cc = 2 * j + oc
                q = cc % 2
                br, bi = gb[(0, oc)], gb[(1, oc)]
                ra = t_ra[:, q, :]; ss = t_s[:, q, :]; iu = t_iu[:, q, :]; hs = t_hs[:, q, :]
                S.op("act", lambda e, ra=ra, br=br, cc=cc: e.activation(out=ra, in_=ps[br][:], func=AF.Sigmoid, bias=vcol(f"brg{l}", cc)),
                     reads=[("ps", br), "vec"], writes=[("ra", q)])
                S.op("act", lambda e, iu=iu, bi=bi, cc=cc: e.activation(out=iu, in_=ps[bi][:], func=AF.Sigmoid, bias=vcol(f"big{l}", cc)),
                     reads=[("ps", bi), "vec"], writes=[("iu", q)])
                S.op("act", lambda e, ra=ra, ss=ss, cc=cc: e.activation(out=ss, in_=ra, func=AF.Exp, scale=cneg2[:, l, cc:cc + 1]),
                     reads=[("ra", q), "cneg"], writes=[("s", q)])
                S.op("act", lambda e, ra=ra, cc=cc: e.activation(out=ra, in_=ra, func=AF.Exp, scale=cneg[:, l, cc:cc + 1]),
                     reads=[("ra", q), "cneg"], writes=[("ra", q)])
                S.op("act", lambda e, ss=ss: e.activation(out=ss, in_=ss, func=AF.Ln, scale=-1.0, bias=1.0 + 1e-7),
                     reads=[("s", q)], writes=[("s", q)])
                S.op("act", lambda e, ss=ss: e.activation(out=ss, in_=ss, func=AF.Exp, scale=0.5),
                     reads=[("s", q)], writes=[("s", q)])
                S.op("dve", lambda e, iu=iu, oc=oc, par=par: e.tensor_tensor(out=iu, in0=iu, in1=xb[:, par, oc, :], op=ALU.mult),
                     reads=[("iu", q), ("xb", par, oc)], writes=[("iu", q)])
                S.op("dve", lambda e, iu=iu, ss=ss: e.tensor_tensor(out=iu, in0=iu, in1=ss, op=ALU.mult),
                     reads=[("iu", q), ("s", q)], writes=[("iu", q)])
                S.op("dve", lambda e, hs=hs, ra=ra, iu=iu, cc=cc: e.tensor_tensor_scan(
                    out=hs, data0=ra, data1=iu, initial=state[:, l, cc:cc + 1], op0=ALU.mult, op1=ALU.add),
                    reads=[("ra", q), ("iu", q), ("st", l, cc)], writes=[("hs", q)])
                S.op("dve", lambda e, hs=hs, cc=cc: e.tensor_copy(out=state[:, l, cc:cc + 1], in_=hs[:, TT - 1:TT]),
                     reads=[("hs", q)], writes=[("st", l, cc)])

        def yproj(j):
            for k2 in range(2):
                cc = 2 * j + k2
                q = cc % 2
                s = load4(w_lin[l, cc])
                b = proj_chunk(s, hb, HK)
                S.op("act", lambda e, b=b, q=q: e.activation(out=t_gy[:, q, :], in_=ps[b][:], func=AF.Gelu_apprx_tanh),
                     reads=[("ps", b)], writes=[("gy", q)])
                S.op("dve", lambda e, cc=cc, q=q: e.tensor_tensor(out=mt[:, cc, :], in0=t_gy[:, q, :], in1=t_hs[:, q, :], op=ALU.mult),
                     reads=[("gy", q), ("hs", q)], writes=mkl[cc])

        xproj(0)
        for j in range(8):
            conv(j)
            if j + 1 < 8:
                xproj(j + 1)
            gb = gates(j)
            gate_math(j, gb)
            if not scan_only:
                yproj(j)
        if scan_only:
            return
        for oc in range(NCH):
            s = load4(w_lout[l, oc])
            b = proj_chunk(s, mt, mkl)
            add_to_resid(b, oc)

    def ffn_layer(l):
        rmsnorm(f"n2g{l}")

        def phase1(g):
            slot = g % 2
            for f in range(GRP):
                fc = g * GRP + f
                s = load8(w_fin[l, fc])
                bg = proj_chunk(s, hb, HK, wring=ring8, woff=0)
                bu = proj_chunk(s, hb, HK, wring=ring8, woff=2048)
                q = fc % 2
                S.op("act", lambda e, bg=bg, q=q: e.activation(out=sqt[:, q, :], in_=ps[bg][:], func=AF.Silu),
                     reads=[("ps", bg)], writes=[("sq", q)])
                S.op("dve", lambda e, bu=bu, q=q, slot=slot, f=f: e.tensor_tensor(
                    out=actb[:, slot, f, :], in0=sqt[:, q, :], in1=ps[bu][:], op=ALU.mult),
                    reads=[("sq", q), ("ps", bu)], writes=[("actb", slot, f)])

        def phase2(g):
            slot = g % 2
            ws = [load4(w_fout[l, g * GRP + f]) for f in range(GRP)]
            for oc in range(NCH):
                b = nb()
                for f in range(GRP):
                    S.op("pe", lambda e, b=b, f=f, oc=oc, slot=slot, ws=ws: e.matmul(
                        ps[b][:], lhsT=ring4[ws[f]][:, oc * 128:(oc + 1) * 128], rhs=actb[:, slot, f, :],
                        start=(f == 0), stop=(f == GRP - 1)),
                        reads=[("r4", ws[f]), ("actb", slot, f)], writes=[("ps", b)])
                add_to_resid(b, oc)

        phase1(0)
        for g in range(NGRP):
            if g + 1 < NGRP:
                phase1(g + 1)
            phase2(g)

    def headnorm(b, gcol_ap, out_ap, out_keys, q):
        S.op("act", lambda e, b=b, q=q: e.activation(out=sqt[:, q, :], in_=ps[b][:], func=AF.Square),
             reads=[("ps", b)], writes=[("sq", q)])
        S.op("act", lambda e, q=q: e.activation(out=sqh[:, q, :], in_=sqt[:, q, :], func=AF.Copy),
             reads=[("sq", q)], writes=[("sqh", q)])
        S.op("dve", lambda e, q=q: e.tensor_tensor(out=sql[:, q, :], in0=sqt[:, q, :], in1=sqh[:, q, :], op=ALU.subtract),
             reads=[("sq", q), ("sqh", q)], writes=[("sql", q)])
        b2 = nb()
        S.op("pe", lambda e, b2=b2, q=q: e.matmul(ps[b2][:], lhsT=blk16[:], rhs=sqh[:, q, :], start=True, stop=False),
             reads=[("sqh", q), "blk16"], writes=[("ps", b2)])
        S.op("pe", lambda e, b2=b2, q=q: e.matmul(ps[b2][:], lhsT=blk16[:], rhs=sql[:, q, :], start=False, stop=True),
             reads=[("sql", q), "blk16"], writes=[("ps", b2)])
        S.op("act", lambda e, b2=b2, q=q: e.activation(out=dent[:, q, :], in_=ps[b2][:], func=AF.Ln, scale=1.0 / 64, bias=EPS),
             reads=[("ps", b2)], writes=[("dent", q)])
        S.op("act", lambda e, q=q: e.activation(out=dent[:, q, :], in_=dent[:, q, :], func=AF.Exp, scale=-0.5),
             reads=[("dent", q)], writes=[("dent", q)])
        S.op("dve", lambda e, b=b, q=q: e.scalar_tensor_tensor(out=out_ap, in0=ps[b][:], scalar=gcol_ap, in1=dent[:, q, :],
                                                              op0=ALU.mult, op1=ALU.mult),
             reads=[("ps", b), ("dent", q), "vec"], writes=out_keys)

    def kv_tile():
        rmsnorm("kvg")
        for h in range(4):
            s = load4(w_k[h])
            b = proj_chunk(s, hb, HK)
            headnorm(b, V("kg", 1), kT2[:, h, 128:128 + TT], [("kT", h)], h % 2)
        s = load8(w_v)
        for tb in range(4):
            b = nb()
            for k in range(NCH):
                S.op("pe", lambda e, b=b, k=k, tb=tb, s=s: e.matmul(
                    ps[b][:, 0:256], lhsT=hb[:, k, tb * 128:(tb + 1) * 128], rhs=ring8[s][:, k * 256:(k + 1) * 256],
                    start=(k == 0), stop=(k == NCH - 1)),
                    reads=[("r8", s), ("hb", k)], writes=[("ps", b)])
            S.op("act", lambda e, b=b, tb=tb: e.activation(out=vtok[:, 1 + tb, :], in_=ps[b][:, 0:256], func=AF.Copy),
                 reads=[("ps", b)], writes=["vt"])

    def kv_shift():
        for h in range(4):
            S.op("pool", lambda e, h=h: e.tensor_copy(out=kT2[:, h, 0:128], in_=kT2[:, h, TT:TT + 128]),
                 reads=[("kT", h)], writes=[("kT", h)])
        S.op("pool", lambda e: e.tensor_copy(out=vtok[:, 0, :], in_=vtok[:, 4, :]), reads=["vt"], writes=["vt"])

    def attn_layer(l, first_tile):
        j = l - 2
        rmsnorm(f"n1g{l}")
        for oc in range(NCH):
            s = load4(w_q[j, oc])
            b = proj_chunk(s, hb, HK)
            headnorm(b, V(f"qg{j}", 1), qT[:, oc, :], [("q", oc)], oc % 2)

        def scores(qb, h, up):
            out = []
            for par in range(2):
                pr = slice(par * 64, par * 64 + 64)
                rhs = qT[pr, h * 4:h * 4 + 4, qb * 128:(qb + 1) * 128]
                lst = []
                for kb in range(2):
                    blk = qb + kb
                    b = nb()
                    S.op("pe", lambda e, b=b, pr=pr, rhs=rhs, blk=blk, h=h: e.matmul(
                        ps[b][:].rearrange("p (a q) -> p a q", a=4), lhsT=kT2[pr, h, blk * 128:(blk + 1) * 128], rhs=rhs,
                        start=True, stop=True),
                        reads=[("kT", h)] + [("q", h * 4 + i) for i in range(4)], writes=[("ps", b)])
                    sl = up * 4 + par * 2 + kb
                    S.op("act", lambda e, b=b, sl=sl: e.activation(out=pT[:, sl, :], in_=ps[b][:], func=AF.Exp, scale=0.125),
                         reads=[("ps", b)], writes=[("pT", sl)])
                    mk = maskc if kb == 1 else (maskpf if (first_tile and qb == 0) else maskp)
                    S.op("dve", lambda e, sl=sl, mk=mk: e.tensor_tensor(out=pT[:, sl, :], in0=pT[:, sl, :], in1=mk[:], op=ALU.mult),
                         reads=[("pT", sl), "maskc", "maskp", "maskpf"], writes=[("pT", sl)])
                    lst.append((blk, sl))
                out.append((par, lst))
            return out

        def pv(qb, h, sc):
            bn = nb()
            bd = nb()
            for par, lst in sc:
                pr = slice(par * 64, par * 64 + 64)
                for i, (blk, sl) in enumerate(lst):
                    S.op("pe", lambda e, bn=bn, pr=pr, blk=blk, sl=sl, i=i, n=len(lst), h=h: e.matmul(
                        ps[bn][pr, :], lhsT=vtok[:, blk, h * 64:(h + 1) * 64], rhs=pT[:, sl, :],
                        start=(i == 0), stop=(i == n - 1)),
                        reads=["vt", ("pT", sl)], writes=[("ps", bn)])
                for i, (blk, sl) in enumerate(lst):
                    S.op("pe", lambda e, bd=bd, pr=pr, sl=sl, i=i, n=len(lst): e.matmul(
                        ps[bd][pr, :], lhsT=ones_bf[:, 0:64], rhs=pT[:, sl, :],
                        start=(i == 0), stop=(i == n - 1)),
                        reads=["ones_bf", ("pT", sl)], writes=[("ps", bd)])
            for par, lst in sc:
                pr = slice(par * 64, par * 64 + 64)
                q = par
                es = esk[pr, j, h * 8 + par:h * 8 + 8:2].unsqueeze(2).broadcast_to([64, 4, 128])
                S.op("dve", lambda e, bd=bd, pr=pr, q=q, es=es: e.tensor_tensor(
                    out=dent[pr, q, :].rearrange("p (a q) -> p a q", a=4),
                    in0=ps[bd][pr, :].rearrange("p (a q) -> p a q", a=4), in1=es, op=ALU.add),
                    reads=[("ps", bd), "esk"], writes=[("dent", q)])
                S.op("dve", lambda e, pr=pr, q=q: e.reciprocal(out=dent[pr, q, :], in_=dent[pr, q, :]),
                     reads=[("dent", q)], writes=[("dent", q)])
                S.op("dve", lambda e, bn=bn, pr=pr, q=q, qb=qb, h=h: e.tensor_tensor(
                    out=mt[pr, h * 4:h * 4 + 4, qb * 128:(qb + 1) * 128],
                    in0=ps[bn][pr, :].rearrange("p (a q) -> p a q", a=4),
                    in1=dent[pr, q, :].rearrange("p (a q) -> p a q", a=4), op=ALU.mult),
                    reads=[("ps", bn), ("dent", q)], writes=[("m", h * 4 + i, qb, par) for i in range(4)])

        units = [(qb, h) for qb in range(4) for h in range(4)]
        sc_cur = scores(*units[0], 0)
        for i, (qb, h) in enumerate(units):
            sc_next = None
            if i + 1 < len(units):
                sc_next = scores(*units[i + 1], (i + 1) % 2)
            pv(qb, h, sc_cur)
            sc_cur = sc_next
        mkl = [mkeys(c) for c in range(NCH)]
        for oc in range(NCH):
            s = load4(w_o[j, oc])
            b = proj_chunk(s, mt, mkl)
            add_to_resid(b, oc)

    def load_tile(ti):
        S.dma("sp", lambda e, ti=ti: e.dma_start(out=resid[:].rearrange("p c t -> p (c t)"), in_=xT[ti]),
              "xin", writes=XK)

    for ti in range(NTP):
        load_tile(ti)
        lru_layer(0)
        ffn_layer(0)
        if ti == NTP - 1 and layers > 2:
            lru_layer(1)
            ffn_layer(1)
            kv_tile()
            kv_shift()
        else:
            lru_layer(1, scan_only=True)
    if NTP > 0:
        fl = V("flag", 1)
        S.op("dve", lambda e: e.tensor_scalar_mul(state[:].rearrange("p l c -> p (l c)"), state[:].rearrange("p l c -> p (l c)"), fl),
             reads=["vec"], writes=[("st", l, c) for l in range(2) for c in range(NCH)])
        S.op("dve", lambda e: e.tensor_scalar_mul(halo[:].rearrange("p l c k -> p (l c k)"), halo[:].rearrange("p l c k -> p (l c k)"), fl),
             reads=["vec"], writes=[("halo", l, c) for l in range(2) for c in range(NCH)])
        for h in range(4):
            S.op("dve", lambda e, h=h: e.tensor_scalar_mul(kT2[:, h, 0:128], kT2[:, h, 0:128], fl),
                 reads=["vec"], writes=[("kT", h)])
        S.op("dve", lambda e: e.tensor_scalar_mul(vtok[:, 0, :], vtok[:, 0, :], fl), reads=["vec"], writes=["vt"])
    out_toks = []
    for ti in range(NTM):
        load_tile(NTP + ti)
        for l in range(layers):
            if l < 2:
                lru_layer(l)
            else:
                if l == 2:
                    kv_tile()
                attn_layer(l, first_tile=(ti == 0))
            ffn_layer(l)
        if layers > 2:
            kv_shift()
        t = S.dma("sp", lambda e, ti=ti: e.dma_start(out=oT[ti], in_=resid[:].rearrange("p c t -> p (c t)")),
                  "xout", reads=XK)
        out_toks.append(t)
    S.wait_all("sp", out_toks)
    S.emit()
    return nc, S


def _chunked(W):
    K, N = W.shape
    a = W.reshape(K // 128, 128, N // 128, 128)
    return np.ascontiguousarray(a.transpose(2, 1, 0, 3)).reshape(N // 128, 128, (K // 128) * 128)


def _v16(v):
    return np.ascontiguousarray(np.asarray(v, np.float32).reshape(16, 128).T)


def prep_weights(inp):
    f = lambda a: np.asarray(a, np.float32)
    vec = np.zeros((128, NV), np.float32)

    def put(name, arr):
        o = VCOL[name]
        vec[:, o:o + arr.shape[1]] = arr
    for l in range(4):
        put(f"n1g{l}", _v16(inp["norm1_g"][l]))
        put(f"n2g{l}", _v16(inp["norm2_g"][l]))
    put("kvg", _v16(inp["kv_norm_g"]))
    for l in range(2):
        for k in range(4):
            put(f"cw{l}_{k}", _v16(inp["lru_conv_w"][l][k]))
        put(f"cb{l}", _v16(inp["lru_conv_b"][l]))
        put(f"brg{l}", _v16(inp["lru_b_rg"][l]))
        put(f"big{l}", _v16(inp["lru_b_ig"][l]))
        put(f"lam{l}", _v16(inp["lru_lambda"][l]))
    put("kg", np.tile(f(inp["k_norm_g"]), 2)[:, None])
    for j in range(2):
        put(f"qg{j}", np.tile(f(inp["q_norm_g"][j]), 2)[:, None])
        put(f"sk{j}", np.tile(f(inp["sinks"][j])[None, :], (128, 1)))
    w = {"vecs": vec}
    w["w_lin"] = np.stack([_chunked(f(inp["lru_w_in"][l])) for l in range(2)])
    w["w_lout"] = np.stack([_chunked(f(inp["lru_w_out"][l])) for l in range(2)])
    rg = f(inp["lru_w_rg"]).reshape(2, 8, 2, 128, 256)
    ig = f(inp["lru_w_ig"]).reshape(2, 8, 2, 128, 256)
    gt = np.stack([rg, ig], axis=3)
    w["w_gate"] = np.ascontiguousarray(gt.transpose(0, 1, 4, 2, 3, 5)).reshape(2, 8, 128, 1024)
    fin = []
    for l in range(4):
        W = f(inp["ffn_w_in"][l])
        g = _chunked(W[:, :DFF])
        u = _chunked(W[:, DFF:])
        fin.append(np.concatenate([g, u], axis=2))
    w["w_fin"] = np.stack(fin)
    w["w_fout"] = np.ascontiguousarray(f(inp["ffn_w_out"]).reshape(4, NFC, 128, 2048))
    wkv = f(inp["w_kv"])
    wk = wkv[:, :256].reshape(2048, 4, 64)
    wk2 = np.concatenate([wk, wk], axis=2)
    w["w_k"] = np.stack([_chunked(np.ascontiguousarray(wk2[:, h, :]))[0] for h in range(4)])
    wv = wkv[:, 256:]
    w["w_v"] = np.ascontiguousarray(wv.reshape(16, 128, 256).transpose(1, 0, 2)).reshape(128, 4096)
    w["w_q"] = np.stack([_chunked(f(inp["w_q"][j])) for j in range(2)])
    w["w_o"] = np.stack([_chunked(f(inp["w_o"][j])) for j in range(2)])
    return w


def to_tiles(xseq, NT):
    a = xseq.reshape(NT, TT, NCH, 128)
    return np.ascontiguousarray(a.transpose(0, 3, 2, 1)).reshape(NT, 128, NCH * TT)


def from_tiles(o, NT):
    a = o.reshape(NT, 128, NCH, TT)
    return np.ascontiguousarray(a.transpose(0, 3, 2, 1)).reshape(NT * TT, D)


_CACHE = {}
NTP = N_TILES // 2
NTM = N_TILES // 2


def kernel(**inputs):
    x = np.asarray(inputs["x"], np.float32)
    B = x.shape[0]
    w = prep_weights(inputs)
    if "nc" not in _CACHE:
        _CACHE["nc"] = build(NTP, NTM)[0]
    nc = _CACHE["nc"]
    in_maps = []
    fo = VCOL["flag"]
    for c in range(2 * B):
        seq, half = c // 2, c % 2
        m = dict(w)
        tiles = to_tiles(x[seq], N_TILES)
        if half == 0:
            xt = np.concatenate([np.zeros_like(tiles[:NTP]), tiles[:NTM]], axis=0)
        else:
            xt = tiles
        v = w["vecs"].copy()
        v[:, fo] = float(half)
        m["vecs"] = v
        m["xT"] = np.ascontiguousarray(xt)
        in_maps.append(m)
    res = run_bass_kernel_spmd(nc, in_maps, core_ids=list(range(2 * B)))
    out = np.zeros((B, SEQ, D), np.float32)
    for c in range(2 * B):
        seq, half = c // 2, c % 2
        out[seq, half * NTM * TT:(half + 1) * NTM * TT] = from_tiles(res.results[c]["oT"], NTM)
    return out
```

```python
import numpy as np
import concourse.bass as bass
import concourse.mybir as mybir
from concourse.bass_utils import run_bass_kernel_spmd

F32 = mybir.dt.float32
BF16 = mybir.dt.bfloat16
AF = mybir.ActivationFunctionType
ALU = mybir.AluOpType

D = 2048
NCH = 16
TT = 512
DFF = 5632
NFC = 44
GRP = 4
NGRP = NFC // GRP
EPS = 1e-6
SEQ = 4096
N_TILES = SEQ // TT
N_CORES = 8


class Sched:
    ENGS = ("pe", "act", "dve", "pool", "sp")

    def __init__(self, nc, same_engine_sync=("act", "dve", "pool")):
        self.nc = nc
        self.ops = {e: [] for e in self.ENGS}
        self.last_w = {}
        self.readers = {}
        self.dma_cnt = {}
        self.same = set(same_engine_sync)

    def _deps(self, reads, writes):
        deps = []
        for b in reads:
            t = self.last_w.get(b)
            if t is not None:
                deps.append(t)
        for b in writes:
            t = self.last_w.get(b)
            if t is not None:
                deps.append(t)
            deps.extend(self.readers.get(b, ()))
        return deps

    def _commit(self, tok, reads, writes):
        for b in writes:
            self.last_w[b] = tok
            self.readers[b] = []
        for b in reads:
            if b in writes:
                continue
            lst = self.readers.setdefault(b, [])
            if tok[0] == "eng":
                for i, t in enumerate(lst):
                    if t[0] == "eng" and t[1] == tok[1]:
                        lst[i] = tok
                        break
                else:
                    lst.append(tok)
            else:
                lst.append(tok)

    def op(self, e, fn, reads=(), writes=()):
        reads = tuple(reads); writes = tuple(writes)
        deps = self._deps(reads, writes)
        idx = len(self.ops[e])
        self.ops[e].append({"fn": fn, "deps": deps, "inc": False, "dma": None})
        tok = ("eng", e, idx)
        self._commit(tok, reads, writes)
        return tok

    def dma(self, q, fn, key, reads=(), writes=()):
        reads = tuple(reads); writes = tuple(writes)
        deps = self._deps(reads, writes)
        self.dma_cnt[key] = self.dma_cnt.get(key, 0) + 16
        tok = ("dma", key, self.dma_cnt[key])
        self.ops[q].append({"fn": fn, "deps": deps, "inc": False, "dma": tok})
        self._commit(tok, reads, writes)
        return tok

    def wait_all(self, e, toks):
        self.ops[e].append({"fn": None, "deps": list(toks), "inc": False, "dma": None})

    def emit(self):
        nc = self.nc
        for e in self.ENGS:
            for rec in self.ops[e]:
                nd = []
                for t in rec["deps"]:
                    if t[0] == "eng":
                        if t[1] == e and e not in self.same:
                            continue
                        self.ops[t[1]][t[2]]["inc"] = True
                    nd.append(t)
                rec["deps"] = nd
        val = {}
        for e in self.ENGS:
            c = 0
            for i, rec in enumerate(self.ops[e]):
                if rec["inc"] and rec["dma"] is None:
                    c += 1
                val[(e, i)] = c
        esem = {e: nc.alloc_semaphore(name=f"s_{e}") for e in self.ENGS}
        dsem = {k: nc.alloc_semaphore(name=f"d_{k}") for k in self.dma_cnt}
        self.n_instr = {e: 0 for e in self.ENGS}
        self.n_wait = {e: 0 for e in self.ENGS}

        def run_engine(e, engine):
            known = {}
            for i, rec in enumerate(self.ops[e]):
                need = {}
                for t in rec["deps"]:
                    if t[0] == "eng":
                        s = esem[t[1]]; v = val[(t[1], t[2])]; kk = ("e", t[1])
                    else:
                        s = dsem[t[1]]; v = t[2]; kk = ("d", t[1])
                    if known.get(kk, 0) >= v:
                        continue
                    if kk not in need or need[kk][1] < v:
                        need[kk] = (s, v)
                for kk, (s, v) in need.items():
                    engine.wait_ge(s, v)
                    known[kk] = v
                    self.n_wait[e] += 1
                if rec["fn"] is None:
                    continue
                ins = rec["fn"](engine)
                self.n_instr[e] += 1
                if rec["dma"] is not None:
                    ins.then_inc(dsem[rec["dma"][1]], 16)
                elif rec["inc"]:
                    ins.then_inc(esem[e], 1)

        with nc.Block() as block:
            @block.tensor
            def _(eng):
                run_engine("pe", eng)

            @block.scalar
            def _(eng):
                run_engine("act", eng)

            @block.vector
            def _(eng):
                run_engine("dve", eng)

            @block.gpsimd
            def _(eng):
                run_engine("pool", eng)

            @block.sync
            def _(eng):
                run_engine("sp", eng)


def _vec_layout():
    cols = {}
    off = 0

    def add(name, w):
        nonlocal off
        cols[name] = off
        off += w
    for l in range(4):
        add(f"n1g{l}", 16)
        add(f"n2g{l}", 16)
    add("kvg", 16)
    for l in range(2):
        for k in range(4):
            add(f"cw{l}_{k}", 16)
        add(f"cb{l}", 16)
        add(f"brg{l}", 16)
        add(f"big{l}", 16)
        add(f"lam{l}", 16)
    add("kg", 1)
    add("qg0", 1)
    add("qg1", 1)
    add("sk0", 16)
    add("sk1", 16)
    add("flag", 1)
    return cols, off


VCOL, NV = _vec_layout()


def build(NTP, NTM, layers=4):
    nc = bass.Bass("TRN2", target_bir_lowering=False)
    S = Sched(nc)

    def dram(name, shape, kind="ExternalInput"):
        return nc.dram_tensor(name, list(shape), F32, kind=kind).ap()

    xT = dram("xT", [NTP + NTM, 128, NCH * TT])
    oT = dram("oT", [NTM, 128, NCH * TT], kind="ExternalOutput")
    vecs_d = dram("vecs", [128, NV])
    w_lin = dram("w_lin", [2, 32, 128, 2048])
    w_lout = dram("w_lout", [2, 16, 128, 2048])
    w_gate = dram("w_gate", [2, 8, 128, 1024])
    w_fin = dram("w_fin", [4, NFC, 128, 4096])
    w_fout = dram("w_fout", [4, NFC, 128, 2048])
    w_k = dram("w_k", [4, 128, 2048])
    w_v = dram("w_v", [128, 4096])
    w_q = dram("w_q", [2, 16, 128, 2048])
    w_o = dram("w_o", [2, 16, 128, 2048])

    sb = nc.alloc_sbuf_tensor
    resid = sb("resid", [128, NCH, TT], F32)
    hb = sb("hb", [128, NCH, TT], BF16)
    mt = sb("mt", [128, NCH, TT], BF16)
    actb = sb("actb", [128, 2, GRP, TT], BF16)
    NR4, NR8 = 6, 3
    ring4 = [sb(f"r4_{i}", [128, 2048], BF16) for i in range(NR4)]
    ring8 = [sb(f"r8_{i}", [128, 4096], BF16) for i in range(NR8)]
    vec = sb("vec", [128, NV], F32)
    ones_bf = sb("ones_bf", [128, 128], BF16)
    blk16 = sb("blk16", [128, 128], BF16)
    ones512 = sb("ones512", [128, 512], BF16)
    maskc = sb("maskc", [128, 512], BF16)
    maskp = sb("maskp", [128, 512], BF16)
    maskpf = sb("maskpf", [128, 512], BF16)
    cneg = sb("cneg", [128, 2, 16], F32)
    cneg2 = sb("cneg2", [128, 2, 16], F32)
    esk = sb("esk", [128, 2, 16], F32)
    hbias = sb("hbias", [128, 2, 2, 16], F32)
    ctmp = sb("ctmp", [128, 4, 32], F32)
    rstd = sb("rstd", [128, TT], F32)
    lnt = rstd
    state = sb("state", [128, 2, 16], F32)
    halo = sb("halo", [128, 2, 16, 4], F32)
    scr = sb("scr", [128, 4 * (TT + 4) + 4 * TT], F32)
    xr = scr[:, 0:4 * (TT + 4)].rearrange("p (a b t) -> p a b t", a=2, b=2)
    xb = scr[:, 4 * (TT + 4):4 * (TT + 4) + 4 * TT].rearrange("p (a b t) -> p a b t", a=2, b=2)
    xbb = sb("xbb", [128, 2, 2, TT], BF16)
    lruA = sb("lruA", [128, 8, TT], F32)
    attn_scr = sb("attn_scr", [128, 8 * TT], F32)

    def ascr(k):
        return attn_scr[:, k * TT:(k + 1) * TT]
    t_ra = [lruA[:, 0, :], lruA[:, 1, :], ascr(0), ascr(1)]
    t_s = [lruA[:, 2, :], lruA[:, 3, :], ascr(2), ascr(3)]
    t_iu = [lruA[:, 4, :], lruA[:, 5, :], ascr(4), ascr(5)]
    t_hs = [lruA[:, 6, :], lruA[:, 7, :], ascr(6), ascr(7)]
    t_y = sb("t_y", [128, 2, TT], F32)
    t_w = sb("t_w", [128, 2, TT], F32)
    qT = scr.bitcast(BF16)[:, 0:NCH * TT].rearrange("p (c t) -> p c t", c=NCH)
    kT2 = sb("kT2", [128, 4, 128 + TT], BF16)
    vtok = sb("vtok", [128, 5, 256], BF16)
    scrb = attn_scr.bitcast(BF16)
    pT = scrb[:, 0:8 * TT].rearrange("p (a t) -> p a t", a=8)
    dent = attn_scr[:, 4 * TT:6 * TT].rearrange("p (a t) -> p a t", a=2)
    sqh = scrb[:, 12 * TT:14 * TT].rearrange("p (a t) -> p a t", a=2)
    sql = scrb[:, 14 * TT:16 * TT].rearrange("p (a t) -> p a t", a=2)
    sqt = sb("sqt", [128, 2, TT], F32)
    ps = [nc.alloc_psum_tensor(f"ps{i}", [128, 512], F32) for i in range(8)]

    cnt = {"ps": 0, "r4": 0, "r8": 0}

    ALLB = (0, 1, 2, 3, 4, 5, 6, 7)
    LO = (0, 1, 2, 3)
    HI = (4, 5, 6, 7)

    def nb(pool="all", banks=ALLB):
        k = "ps_" + pool
        i = cnt.get(k, 0)
        cnt[k] = i + 1
        return banks[i % len(banks)]

    XK = [("x", c) for c in range(NCH)]
    HK = [("hb", c) for c in range(NCH)]

    def mkeys(c):
        return [("m", c, qb, par) for qb in range(4) for par in range(2)]
    MK = [k for c in range(NCH) for k in mkeys(c)]
    QK = [("q", c) for c in range(NCH)]

    def V(name, w=16):
        o = VCOL[name]
        return vec[:, o:o + w]

    def vcol(name, c):
        o = VCOL[name] + c
        return vec[:, o:o + 1]

    def load4(src, ncols=2048):
        s = cnt["r4"] % NR4
        cnt["r4"] += 1
        dst = ring4[s][:, 0:ncols]
        S.dma("pool", lambda e: e.dma_start(out=dst, in_=src), f"r4_{s}", writes=[("r4", s)])
        return s

    def load8(src):
        s = cnt["r8"] % NR8
        cnt["r8"] += 1
        S.dma("pool", lambda e: e.dma_start(out=ring8[s][:].rearrange("p (a b) -> p a b", b=2048),
                                            in_=src.rearrange("p (a b) -> p a b", b=2048)),
              f"r8_{s}", writes=[("r8", s)])
        return s

    S.dma("sp", lambda e: e.dma_start(out=vec[:], in_=vecs_d[:, :]), "vec", writes=["vec"])
    S.op("dve", lambda e: e.memset(ones_bf[:], 1.0), writes=["ones_bf"])
    S.op("dve", lambda e: e.memset(ones512[:], 1.0), writes=["ones512"])
    S.op("dve", lambda e: e.memset(blk16[:], 0.0), writes=["blk16"])
    S.op("dve", lambda e: e.memset(blk16[0:64, 0:64], 1.0), writes=["blk16"])
    S.op("dve", lambda e: e.memset(blk16[64:128, 64:128], 1.0), writes=["blk16"])
    S.op("dve", lambda e: e.memset(state[:], 0.0), writes=[("st", l, c) for l in range(2) for c in range(NCH)])
    S.op("dve", lambda e: e.memset(halo[:], 0.0), writes=[("halo", l, c) for l in range(2) for c in range(NCH)])
    S.op("dve", lambda e: e.memset(kT2[:], 0.0), writes=[("kT", h) for h in range(4)])
    S.op("dve", lambda e: e.memset(vtok[:], 0.0), writes=["vt"])
    S.op("pool", lambda e: e.affine_select(out=maskc[:], in_=ones512[:], pattern=[[0, 4], [1, 128]],
                                           compare_op=ALU.is_ge, fill=0.0, base=0, channel_multiplier=-1),
         reads=["ones512"], writes=["maskc"])
    S.op("pool", lambda e: e.affine_select(out=maskp[:], in_=ones512[:], pattern=[[0, 4], [-1, 128]],
                                           compare_op=ALU.is_ge, fill=0.0, base=-1, channel_multiplier=1),
         reads=["ones512"], writes=["maskp"])
    S.op("dve", lambda e: e.tensor_scalar_mul(maskpf[:], maskp[:], V("flag", 1)), reads=["maskp", "vec"], writes=["maskpf"])
    for l in range(2):
        lam = V(f"lam{l}")
        z = ctmp[:, 0, 0:16]; az = ctmp[:, 1, 0:16]; ee = ctmp[:, 2, 0:16]; rz = ctmp[:, 3, 0:16]
        S.op("dve", lambda e, z=z, lam=lam: e.tensor_scalar_mul(z, lam, -1.0), reads=["vec"], writes=["ctmp"])
        S.op("dve", lambda e, z=z, lam=lam, az=az: e.tensor_tensor(az, z, lam, ALU.max), reads=["vec", "ctmp"], writes=["ctmp"])
        S.op("act", lambda e, az=az, ee=ee: e.activation(out=ee, in_=az, func=AF.Exp, scale=-1.0), reads=["ctmp"], writes=["ctmp"])
        S.op("act", lambda e, ee=ee: e.activation(out=ee, in_=ee, func=AF.Ln, bias=1.0), reads=["ctmp"], writes=["ctmp"])
        S.op("dve", lambda e, z=z, rz=rz: e.tensor_scalar_max(rz, z, 0.0), reads=["ctmp"], writes=["ctmp"])
        S.op("dve", lambda e, rz=rz, ee=ee: e.tensor_tensor(rz, rz, ee, ALU.add), reads=["ctmp"], writes=["ctmp"])
        S.op("dve", lambda e, rz=rz, l=l: e.tensor_scalar_mul(cneg[:, l, :], rz, -8.0), reads=["ctmp"], writes=["cneg"])
        S.op("dve", lambda e, rz=rz, l=l: e.tensor_scalar_mul(cneg2[:, l, :], rz, -4.0), reads=["ctmp"], writes=["cneg"])
    for j in range(2):
        S.op("act", lambda e, j=j: e.activation(out=esk[:, j, :], in_=V(f"sk{j}", 16), func=AF.Exp), reads=["vec"], writes=["esk"])
    for l in range(2):
        S.op("dve", lambda e, l=l: e.tensor_scalar_mul(hbias[:, l, 0, :], V(f"brg{l}"), 0.5), reads=["vec"], writes=["hbias"])
        S.op("dve", lambda e, l=l: e.tensor_scalar_mul(hbias[:, l, 1, :], V(f"big{l}"), 0.5), reads=["vec"], writes=["hbias"])

    def rmsnorm(gname):
        S.op("act", lambda e: e.activation(out=hb[:].rearrange("p c t -> p (c t)"),
                                           in_=resid[:].rearrange("p c t -> p (c t)"), func=AF.Square),
             reads=XK, writes=HK)
        b = nb()
        for c in range(NCH):
            S.op("pe", lambda e, c=c, b=b: e.matmul(ps[b][:], lhsT=ones_bf[:], rhs=hb[:, c, :],
                                                   start=(c == 0), stop=(c == NCH - 1)),
                 reads=[("hb", c), "ones_bf"], writes=[("ps", b)])
        S.op("act", lambda e, b=b: e.activation(out=lnt[:], in_=ps[b][:], func=AF.Ln, scale=1.0 / D, bias=EPS),
             reads=[("ps", b)], writes=["lnt"])
        S.op("act", lambda e: e.activation(out=rstd[:], in_=lnt[:], func=AF.Exp, scale=-0.5),
             reads=["lnt"], writes=["rstd"])
        for c in range(NCH):
            S.op("dve", lambda e, c=c: e.scalar_tensor_tensor(out=hb[:, c, :], in0=resid[:, c, :],
                                                             scalar=vcol(gname, c), in1=rstd[:],
                                                             op0=ALU.mult, op1=ALU.mult),
                 reads=[("x", c), "rstd", "vec"], writes=[("hb", c)])

    def proj_chunk(wslot, src, srckeys, wring=None, woff=0, pool=("all", ALLB)):
        b = nb(*pool)
        wr = ring4 if wring is None else wring
        wkey = ("r4", wslot) if wring is None else ("r8", wslot)
        for k in range(NCH):
            S.op("pe", lambda e, k=k, b=b: e.matmul(ps[b][:], lhsT=wr[wslot][:, woff + k * 128: woff + (k + 1) * 128],
                                                   rhs=src[:, k, :], start=(k == 0), stop=(k == NCH - 1)),
                 reads=[wkey, srckeys[k]] if not isinstance(srckeys[k], list) else [wkey] + srckeys[k],
                 writes=[("ps", b)])
        return b

    def add_to_resid(b, oc):
        S.op("dve", lambda e, b=b, oc=oc: e.tensor_tensor(out=resid[:, oc, :], in0=resid[:, oc, :], in1=ps[b][:], op=ALU.add),
             reads=[("ps", b)], writes=[("x", oc)])

    def lru_layer(l, scan_only=False):
        rmsnorm(f"n1g{l}")
        mkl = [mkeys(c) for c in range(NCH)]

        def xproj(j):
            par = j % 2
            banks = []
            for k2 in range(2):
                cc = 2 * j + k2
                s = load4(w_lin[l, 16 + cc])
                b = proj_chunk(s, hb, HK, pool=("lx", (0, 1)))
                banks.append(b)
                S.op("dve", lambda e, cc=cc, k2=k2, par=par: e.tensor_copy(out=xr[:, par, k2, 0:3], in_=halo[:, l, cc, 0:3]),
                     reads=[("halo", l, cc)], writes=[("xr", par, k2)])
                S.op("dve", lambda e, b=b, k2=k2, par=par: e.tensor_copy(out=xr[:, par, k2, 3:3 + TT], in_=ps[b][:]),
                     reads=[("ps", b)], writes=[("xr", par, k2)])
            return banks

        def conv(j):
            par = j % 2
            for k2 in range(2):
                cc = 2 * j + k2
                o = xb[:, par, k2, :]
                S.op("dve", lambda e, o=o, cc=cc, k2=k2, par=par: e.tensor_scalar(
                    o, xr[:, par, k2, 3:3 + TT], vcol(f"cw{l}_3", cc), vcol(f"cb{l}", cc), ALU.mult, ALU.add),
                    reads=[("xr", par, k2), "vec"], writes=[("xb", par, k2)])
                for k in range(3):
                    S.op("dve", lambda e, o=o, cc=cc, k2=k2, par=par, k=k: e.scalar_tensor_tensor(
                        out=o, in0=xr[:, par, k2, k:k + TT], scalar=vcol(f"cw{l}_{k}", cc), in1=o,
                        op0=ALU.mult, op1=ALU.add),
                        reads=[("xr", par, k2), "vec"], writes=[("xb", par, k2)])
                S.op("act", lambda e, o=o, k2=k2, par=par: e.activation(out=xbb[:, par, k2, :], in_=o, func=AF.Copy),
                     reads=[("xb", par, k2)], writes=[("xbb", par, k2)])
                S.op("dve", lambda e, cc=cc, k2=k2, par=par: e.tensor_copy(out=halo[:, l, cc, 0:3], in_=xr[:, par, k2, TT:TT + 3]),
                     reads=[("xr", par, k2)], writes=[("halo", l, cc)])

        def gates(j):
            par = j % 2
            s = load4(w_gate[l, j], ncols=1024)
            res = {}
            for g in range(2):
                for oc in range(2):
                    b = nb("lg", (2, 3, 4, 5))
                    for ic in range(2):
                        off = ic * 512 + g * 256 + oc * 128
                        S.op("pe", lambda e, b=b, ic=ic, off=off, s=s, par=par: e.matmul(
                            ps[b][:], lhsT=ring4[s][:, off:off + 128], rhs=xbb[:, par, ic, :],
                            start=(ic == 0), stop=(ic == 1)),
                            reads=[("r4", s), ("xbb", par, ic)], writes=[("ps", b)])
                    res[(g, oc)] = b
            return res

        def gate_math(j, gb):
            par = j % 2
            for oc in range(2):
                cc = 2 * j + oc
                q = cc % 4
                br, bi = gb[(0, oc)], gb[(1, oc)]
                S.op("act", lambda e, q=q, br=br, cc=cc: e.activation(out=t_ra[q], in_=ps[br][:], func=AF.Tanh, scale=0.5,
                                                                   bias=hbias[:, l, 0, cc:cc + 1]),
                     reads=[("ps", br), "hbias"], writes=[("ra", q)])
                S.op("act", lambda e, q=q, bi=bi, cc=cc: e.activation(out=t_iu[q], in_=ps[bi][:], func=AF.Tanh, scale=0.5,
                                                                   bias=hbias[:, l, 1, cc:cc + 1]),
                     reads=[("ps", bi), "hbias"], writes=[("iu", q)])
            for oc in range(2):
                cc = 2 * j + oc
                q = cc % 4
                S.op("act", lambda e, q=q, cc=cc: e.activation(out=t_s[q], in_=t_ra[q], func=AF.Exp,
                                                             scale=cneg[:, l, cc:cc + 1], bias=cneg[:, l, cc:cc + 1]),
                     reads=[("ra", q), "cneg"], writes=[("s", q)])
                S.op("act", lambda e, q=q, cc=cc: e.activation(out=t_ra[q], in_=t_ra[q], func=AF.Exp,
                                                             scale=cneg2[:, l, cc:cc + 1], bias=cneg2[:, l, cc:cc + 1]),
                     reads=[("ra", q), "cneg"], writes=[("ra", q)])
            for oc in range(2):
                q = (2 * j + oc) % 4
                S.op("act", lambda e, q=q: e.activation(out=t_s[q], in_=t_s[q], func=AF.Ln, scale=-1.0, bias=1.0 + 1e-7),
                     reads=[("s", q)], writes=[("s", q)])
            for oc in range(2):
                q = (2 * j + oc) % 4
                S.op("act", lambda e, q=q: e.activation(out=t_s[q], in_=t_s[q], func=AF.Exp, scale=0.5),
                     reads=[("s", q)], writes=[("s", q)])
            for oc in range(2):
                cc = 2 * j + oc
                q = cc % 4
                iu = t_iu[q]; hs = t_hs[q]
                S.op("dve", lambda e, iu=iu, oc=oc, par=par: e.scalar_tensor_tensor(out=iu, in0=iu, scalar=1.0, in1=xb[:, par, oc, :],
                                                                                  op0=ALU.add, op1=ALU.mult),
                     reads=[("iu", q), ("xb", par, oc)], writes=[("iu", q)])
                S.op("dve", lambda e, iu=iu, q=q: e.scalar_tensor_tensor(out=iu, in0=iu, scalar=0.5, in1=t_s[q],
                                                                      op0=ALU.mult, op1=ALU.mult),
                     reads=[("iu", q), ("s", q)], writes=[("iu", q)])
                S.op("dve", lambda e, hs=hs, iu=iu, cc=cc, q=q: e.tensor_tensor_scan(
                    out=hs, data0=t_ra[q], data1=iu, initial=state[:, l, cc:cc + 1], op0=ALU.mult, op1=ALU.add),
                    reads=[("ra", q), ("iu", q), ("st", l, cc)], writes=[("hs", q)])
                S.op("dve", lambda e, hs=hs, cc=cc: e.tensor_copy(out=state[:, l, cc:cc + 1], in_=hs[:, TT - 1:TT]),
                     reads=[("hs", q)], writes=[("st", l, cc)])

        def yproj(j):
            for k2 in range(2):
                cc = 2 * j + k2
                q = cc % 4
                q2 = cc % 2
                s = load4(w_lin[l, cc])
                b = proj_chunk(s, hb, HK, pool=("ly", (6, 7)))
                ty = t_y[:, q2, :]; tw = t_w[:, q2, :]
                S.op("act", lambda e, b=b, ty=ty: e.activation(out=ty, in_=ps[b][:], func=AF.Copy),
                     reads=[("ps", b)], writes=[("ty", q2)])
                S.op("dve", lambda e, ty=ty, tw=tw: e.tensor_tensor(out=tw, in0=ty, in1=ty, op=ALU.mult),
                     reads=[("ty", q2)], writes=[("tw", q2)])
                S.op("dve", lambda e, tw=tw: e.tensor_scalar(tw, tw, 0.044715, 1.0, ALU.mult, ALU.add),
                     reads=[("tw", q2)], writes=[("tw", q2)])
                S.op("dve", lambda e, ty=ty, tw=tw: e.tensor_tensor(out=tw, in0=tw, in1=ty, op=ALU.mult),
                     reads=[("tw", q2), ("ty", q2)], writes=[("tw", q2)])
                S.op("act", lambda e, tw=tw: e.activation(out=tw, in_=tw, func=AF.Tanh, scale=0.7978845608028654),
                     reads=[("tw", q2)], writes=[("tw", q2)])
                S.op("dve", lambda e, ty=ty, tw=tw: e.scalar_tensor_tensor(out=tw, in0=tw, scalar=1.0, in1=ty, op0=ALU.add, op1=ALU.mult),
                     reads=[("tw", q2), ("ty", q2)], writes=[("tw", q2)])
                S.op("dve", lambda e, tw=tw, cc=cc, q=q: e.scalar_tensor_tensor(out=mt[:, cc, :], in0=tw, scalar=0.5, in1=t_hs[q],
                                                                              op0=ALU.mult, op1=ALU.mult),
                     reads=[("tw", q2), ("hs", q)], writes=mkl[cc])

        xproj(0)
        conv(0)
        xproj(1)
        for j in range(8):
            gb = gates(j)
            if j + 1 < 8:
                conv(j + 1)
            if j + 2 < 8:
                xproj(j + 2)
            gate_math(j, gb)
            if not scan_only:
                yproj(j)
        if scan_only:
            return
        for oc in range(NCH):
            s = load4(w_lout[l, oc])
            b = proj_chunk(s, mt, mkl)
            add_to_resid(b, oc)

    def ffn_layer(l):
        rmsnorm(f"n2g{l}")

        def phase1(g):
            slot = g % 2
            for f in range(GRP):
                fc = g * GRP + f
                s = load8(w_fin[l, fc])
                bg = proj_chunk(s, hb, HK, wring=ring8, woff=0, pool=("lo", LO))
                bu = proj_chunk(s, hb, HK, wring=ring8, woff=2048, pool=("lo", LO))
                q = fc % 2
                S.op("act", lambda e, bg=bg, q=q: e.activation(out=sqt[:, q, :], in_=ps[bg][:], func=AF.Silu),
                     reads=[("ps", bg)], writes=[("sq", q)])
                S.op("dve", lambda e, bu=bu, q=q, slot=slot, f=f: e.tensor_tensor(
                    out=actb[:, slot, f, :], in0=sqt[:, q, :], in1=ps[bu][:], op=ALU.mult),
                    reads=[("sq", q), ("ps", bu)], writes=[("actb", slot, f)])

        def phase2(g):
            slot = g % 2
            ws = [load4(w_fout[l, g * GRP + f]) for f in range(GRP)]
            for oc in range(NCH):
                b = nb("hi", HI)
                for f in range(GRP):
                    S.op("pe", lambda e, b=b, f=f, oc=oc, slot=slot, ws=ws: e.matmul(
                        ps[b][:], lhsT=ring4[ws[f]][:, oc * 128:(oc + 1) * 128], rhs=actb[:, slot, f, :],
                        start=(f == 0), stop=(f == GRP - 1)),
                        reads=[("r4", ws[f]), ("actb", slot, f)], writes=[("ps", b)])
                add_to_resid(b, oc)

        phase1(0)
        for g in range(NGRP):
            if g + 1 < NGRP:
                phase1(g + 1)
            phase2(g)

    def headnorm(b, gcol_ap, out_ap, out_keys, q):
        S.op("act", lambda e, b=b, q=q: e.activation(out=sqt[:, q, :], in_=ps[b][:], func=AF.Square),
             reads=[("ps", b)], writes=[("sq", q)])
        S.op("act", lambda e, q=q: e.activation(out=sqh[:, q, :], in_=sqt[:, q, :], func=AF.Copy),
             reads=[("sq", q)], writes=[("sqh", q)])
        S.op("dve", lambda e, q=q: e.tensor_tensor(out=sql[:, q, :], in0=sqt[:, q, :], in1=sqh[:, q, :], op=ALU.subtract),
             reads=[("sq", q), ("sqh", q)], writes=[("sql", q)])
        b2 = nb("hi", HI)
        S.op("pe", lambda e, b2=b2, q=q: e.matmul(ps[b2][:], lhsT=blk16[:], rhs=sqh[:, q, :], start=True, stop=False),
             reads=[("sqh", q), "blk16"], writes=[("ps", b2)])
        S.op("pe", lambda e, b2=b2, q=q: e.matmul(ps[b2][:], lhsT=blk16[:], rhs=sql[:, q, :], start=False, stop=True),
             reads=[("sql", q), "blk16"], writes=[("ps", b2)])
        S.op("act", lambda e, b2=b2, q=q: e.activation(out=dent[:, q, :], in_=ps[b2][:], func=AF.Ln, scale=1.0 / 64, bias=EPS),
             reads=[("ps", b2)], writes=[("dent", q)])
        S.op("act", lambda e, q=q: e.activation(out=dent[:, q, :], in_=dent[:, q, :], func=AF.Exp, scale=-0.5),
             reads=[("dent", q)], writes=[("dent", q)])
        S.op("dve", lambda e, b=b, q=q: e.scalar_tensor_tensor(out=out_ap, in0=ps[b][:], scalar=gcol_ap, in1=dent[:, q, :],
                                                              op0=ALU.mult, op1=ALU.mult),
             reads=[("ps", b), ("dent", q), "vec"], writes=out_keys)

    def kv_tile():
        rmsnorm("kvg")
        for h in range(4):
            s = load4(w_k[h])
            b = proj_chunk(s, hb, HK, pool=("lo", LO))
            headnorm(b, V("kg", 1), kT2[:, h, 128:128 + TT], [("kT", h)], h % 2)
        s = load8(w_v)
        for tb in range(4):
            b = nb()
            for k in range(NCH):
                S.op("pe", lambda e, b=b, k=k, tb=tb, s=s: e.matmul(
                    ps[b][:, 0:256], lhsT=hb[:, k, tb * 128:(tb + 1) * 128], rhs=ring8[s][:, k * 256:(k + 1) * 256],
                    start=(k == 0), stop=(k == NCH - 1)),
                    reads=[("r8", s), ("hb", k)], writes=[("ps", b)])
            S.op("act", lambda e, b=b, tb=tb: e.activation(out=vtok[:, 1 + tb, :], in_=ps[b][:, 0:256], func=AF.Copy),
                 reads=[("ps", b)], writes=["vt"])

    def kv_shift():
        for h in range(4):
            S.op("pool", lambda e, h=h: e.tensor_copy(out=kT2[:, h, 0:128], in_=kT2[:, h, TT:TT + 128]),
                 reads=[("kT", h)], writes=[("kT", h)])
        S.op("pool", lambda e: e.tensor_copy(out=vtok[:, 0, :], in_=vtok[:, 4, :]), reads=["vt"], writes=["vt"])

    def attn_layer(l, first_tile):
        j = l - 2
        rmsnorm(f"n1g{l}")
        for oc in range(NCH):
            s = load4(w_q[j, oc])
            b = proj_chunk(s, hb, HK, pool=("lo", LO))
            headnorm(b, V(f"qg{j}", 1), qT[:, oc, :], [("q", oc)], oc % 2)

        def scores(qb, h, up):
            out = []
            for par in range(2):
                pr = slice(par * 64, par * 64 + 64)
                rhs = qT[pr, h * 4:h * 4 + 4, qb * 128:(qb + 1) * 128]
                lst = []
                for kb in range(2):
                    blk = qb + kb
                    b = nb("lo", LO)
                    S.op("pe", lambda e, b=b, pr=pr, rhs=rhs, blk=blk, h=h: e.matmul(
                        ps[b][:].rearrange("p (a q) -> p a q", a=4), lhsT=kT2[pr, h, blk * 128:(blk + 1) * 128], rhs=rhs,
                        start=True, stop=True),
                        reads=[("kT", h)] + [("q", h * 4 + i) for i in range(4)], writes=[("ps", b)])
                    sl = up * 4 + par * 2 + kb
                    S.op("act", lambda e, b=b, sl=sl: e.activation(out=pT[:, sl, :], in_=ps[b][:], func=AF.Exp, scale=0.125),
                         reads=[("ps", b)], writes=[("pT", sl)])
                    mk = maskc if kb == 1 else (maskpf if (first_tile and qb == 0) else maskp)
                    S.op("pool", lambda e, sl=sl, mk=mk: e.tensor_tensor(out=pT[:, sl, :], in0=pT[:, sl, :], in1=mk[:], op=ALU.mult),
                         reads=[("pT", sl), "maskc", "maskp", "maskpf"], writes=[("pT", sl)])
                    lst.append((blk, sl))
                out.append((par, lst))
            return out

        def pv(qb, h, sc):
            bn = nb("hi", HI)
            bd = nb("hi", HI)
            for par, lst in sc:
                pr = slice(par * 64, par * 64 + 64)
                for i, (blk, sl) in enumerate(lst):
                    S.op("pe", lambda e, bn=bn, pr=pr, blk=blk, sl=sl, i=i, n=len(lst), h=h: e.matmul(
                        ps[bn][pr, :], lhsT=vtok[:, blk, h * 64:(h + 1) * 64], rhs=pT[:, sl, :],
                        start=(i == 0), stop=(i == n - 1)),
                        reads=["vt", ("pT", sl)], writes=[("ps", bn)])
                for i, (blk, sl) in enumerate(lst):
                    S.op("pe", lambda e, bd=bd, pr=pr, sl=sl, i=i, n=len(lst): e.matmul(
                        ps[bd][pr, :], lhsT=ones_bf[:, 0:64], rhs=pT[:, sl, :],
                        start=(i == 0), stop=(i == n - 1)),
                        reads=["ones_bf", ("pT", sl)], writes=[("ps", bd)])
            q = (qb * 4 + h) % 2
            es = esk[:, j, h * 4:h * 4 + 4].unsqueeze(2).broadcast_to([128, 4, 128])
            S.op("dve", lambda e, bd=bd, q=q, es=es: e.tensor_tensor(
                out=dent[:, q, :].rearrange("p (a q) -> p a q", a=4),
                in0=ps[bd][:].rearrange("p (a q) -> p a q", a=4), in1=es, op=ALU.add),
                reads=[("ps", bd), "esk"], writes=[("dent", q)])
            S.op("dve", lambda e, q=q: e.reciprocal(out=dent[:, q, :], in_=dent[:, q, :]),
                 reads=[("dent", q)], writes=[("dent", q)])
            S.op("dve", lambda e, bn=bn, q=q, qb=qb, h=h: e.tensor_tensor(
                out=mt[:, h * 4:h * 4 + 4, qb * 128:(qb + 1) * 128],
                in0=ps[bn][:].rearrange("p (a q) -> p a q", a=4),
                in1=dent[:, q, :].rearrange("p (a q) -> p a q", a=4), op=ALU.mult),
                reads=[("ps", bn), ("dent", q)], writes=[("m", h * 4 + i, qb, par) for i in range(4) for par in range(2)])

        units = [(qb, h) for qb in range(4) for h in range(4)]
        NPRE = NR4 - 1
        wo_slots = [load4(w_o[j, oc]) for oc in range(NPRE)]
        sc_cur = scores(*units[0], 0)
        for i, (qb, h) in enumerate(units):
            sc_next = None
            if i + 1 < len(units):
                sc_next = scores(*units[i + 1], (i + 1) % 2)
            pv(qb, h, sc_cur)
            sc_cur = sc_next
        mkl = [mkeys(c) for c in range(NCH)]
        for oc in range(NCH):
            s = wo_slots[oc] if oc < NPRE else load4(w_o[j, oc])
            b = proj_chunk(s, mt, mkl)
            add_to_resid(b, oc)

    def load_tile(ti):
        S.dma("sp", lambda e, ti=ti: e.dma_start(out=resid[:].rearrange("p c t -> p (c t)"), in_=xT[ti]),
              "xin", writes=XK)

    for ti in range(NTP):
        load_tile(ti)
        lru_layer(0)
        ffn_layer(0)
        if ti == NTP - 1 and layers > 2:
            lru_layer(1)
            ffn_layer(1)
            kv_tile()
            kv_shift()
        else:
            lru_layer(1, scan_only=True)
    if NTP > 0:
        fl = V("flag", 1)
        S.op("dve", lambda e: e.tensor_scalar_mul(state[:].rearrange("p l c -> p (l c)"), state[:].rearrange("p l c -> p (l c)"), fl),
             reads=["vec"], writes=[("st", l, c) for l in range(2) for c in range(NCH)])
        S.op("dve", lambda e: e.tensor_scalar_mul(halo[:].rearrange("p l c k -> p (l c k)"), halo[:].rearrange("p l c k -> p (l c k)"), fl),
             reads=["vec"], writes=[("halo", l, c) for l in range(2) for c in range(NCH)])
        for h in range(4):
            S.op("dve", lambda e, h=h: e.tensor_scalar_mul(kT2[:, h, 0:128], kT2[:, h, 0:128], fl),
                 reads=["vec"], writes=[("kT", h)])
        S.op("dve", lambda e: e.tensor_scalar_mul(vtok[:, 0, :], vtok[:, 0, :], fl), reads=["vec"], writes=["vt"])
    out_toks = []
    for ti in range(NTM):
        load_tile(NTP + ti)
        for l in range(layers):
            if l < 2:
                lru_layer(l)
            else:
                if l == 2:
                    kv_tile()
                attn_layer(l, first_tile=(ti == 0))
            ffn_layer(l)
        if layers > 2:
            kv_shift()
        t = S.dma("sp", lambda e, ti=ti: e.dma_start(out=oT[ti], in_=resid[:].rearrange("p c t -> p (c t)")),
                  "xout", reads=XK)
        out_toks.append(t)
    S.wait_all("sp", out_toks)
    S.emit()
    return nc, S


def _chunked(W):
    K, N = W.shape
    a = W.reshape(K // 128, 128, N // 128, 128)
    return np.ascontiguousarray(a.transpose(2, 1, 0, 3)).reshape(N // 128, 128, (K // 128) * 128)


def _v16(v):
    return np.ascontiguousarray(np.asarray(v, np.float32).reshape(16, 128).T)


def prep_weights(inp):
    f = lambda a: np.asarray(a, np.float32)
    vec = np.zeros((128, NV), np.float32)

    def put(name, arr):
        o = VCOL[name]
        vec[:, o:o + arr.shape[1]] = arr
    for l in range(4):
        put(f"n1g{l}", _v16(inp["norm1_g"][l]))
        put(f"n2g{l}", _v16(inp["norm2_g"][l]))
    put("kvg", _v16(inp["kv_norm_g"]))
    for l in range(2):
        for k in range(4):
            put(f"cw{l}_{k}", _v16(inp["lru_conv_w"][l][k]))
        put(f"cb{l}", _v16(inp["lru_conv_b"][l]))
        put(f"brg{l}", _v16(inp["lru_b_rg"][l]))
        put(f"big{l}", _v16(inp["lru_b_ig"][l]))
        put(f"lam{l}", _v16(inp["lru_lambda"][l]))
    put("kg", np.tile(f(inp["k_norm_g"]), 2)[:, None])
    for j in range(2):
        put(f"qg{j}", np.tile(f(inp["q_norm_g"][j]), 2)[:, None])
        sk = f(inp["sinks"][j]).reshape(4, 4, 2)
        put(f"sk{j}", np.concatenate([np.tile(sk[:, :, 0].reshape(1, 16), (64, 1)),
                                      np.tile(sk[:, :, 1].reshape(1, 16), (64, 1))], axis=0))
    w = {"vecs": vec}
    w["w_lin"] = np.stack([_chunked(f(inp["lru_w_in"][l])) for l in range(2)])
    w["w_lout"] = np.stack([_chunked(f(inp["lru_w_out"][l])) for l in range(2)])
    rg = f(inp["lru_w_rg"]).reshape(2, 8, 2, 128, 256)
    ig = f(inp["lru_w_ig"]).reshape(2, 8, 2, 128, 256)
    gt = np.stack([rg, ig], axis=3)
    w["w_gate"] = np.ascontiguousarray(gt.transpose(0, 1, 4, 2, 3, 5)).reshape(2, 8, 128, 1024)
    fin = []
    for l in range(4):
        W = f(inp["ffn_w_in"][l])
        g = _chunked(W[:, :DFF])
        u = _chunked(W[:, DFF:])
        fin.append(np.concatenate([g, u], axis=2))
    w["w_fin"] = np.stack(fin)
    w["w_fout"] = np.ascontiguousarray(f(inp["ffn_w_out"]).reshape(4, NFC, 128, 2048))
    wkv = f(inp["w_kv"])
    wk = wkv[:, :256].reshape(2048, 4, 64)
    wk2 = np.concatenate([wk, wk], axis=2)
    w["w_k"] = np.stack([_chunked(np.ascontiguousarray(wk2[:, h, :]))[0] for h in range(4)])
    wv = wkv[:, 256:]
    w["w_v"] = np.ascontiguousarray(wv.reshape(16, 128, 256).transpose(1, 0, 2)).reshape(128, 4096)
    w["w_q"] = np.stack([_chunked(f(inp["w_q"][j])) for j in range(2)])
    w["w_o"] = np.stack([_chunked(f(inp["w_o"][j])) for j in range(2)])
    return w


def to_tiles(xseq, NT):
    a = xseq.reshape(NT, TT, NCH, 128)
    return np.ascontiguousarray(a.transpose(0, 3, 2, 1)).reshape(NT, 128, NCH * TT)


def from_tiles(o, NT):
    a = o.reshape(NT, 128, NCH, TT)
    return np.ascontiguousarray(a.transpose(0, 3, 2, 1)).reshape(NT * TT, D)


_CACHE = {}
NTP = N_TILES // 2
NTM = N_TILES // 2


def kernel(**inputs):
    x = np.asarray(inputs["x"], np.float32)
    B = x.shape[0]
    w = prep_weights(inputs)
    if "nc" not in _CACHE:
        _CACHE["nc"] = build(NTP, NTM)[0]
    nc = _CACHE["nc"]
    in_maps = []
    fo = VCOL["flag"]
    for c in range(2 * B):
        seq, half = c // 2, c % 2
        m = dict(w)
        tiles = to_tiles(x[seq], N_TILES)
        if half == 0:
            xt = np.concatenate([np.zeros_like(tiles[:NTP]), tiles[:NTM]], axis=0)
        else:
            xt = tiles
        v = w["vecs"].copy()
        v[:, fo] = float(half)
        m["vecs"] = v
        m["xT"] = np.ascontiguousarray(xt)
        in_maps.append(m)
    res = run_bass_kernel_spmd(nc, in_maps, core_ids=list(range(2 * B)))
    out = np.zeros((B, SEQ, D), np.float32)
    for c in range(2 * B):
        seq, half = c // 2, c % 2
        out[seq, half * NTM * TT:(half + 1) * NTM * TT] = from_tiles(res.results[c]["oT"], NTM)
    return out
```
